# Optimizing a Trainium2 kernel written in Bass

```python
import jax, jax.numpy as jnp
from jax import lax
import numpy as np


D_MODEL = 1024
BATCH = 8
SEQ = 4096
DEPTH = 1

MLSTM_HEADS = 4
MLSTM_DIM = D_MODEL
MLSTM_HEAD_DIM = MLSTM_DIM // MLSTM_HEADS
MLSTM_CHUNK = 64
CONV_WIDTH = 3
SGU_DIM = D_MODEL
SGU_GROUPS = 8
SGU_GROUP_DIM = SGU_DIM // SGU_GROUPS
SGU_CHUNK = 128
FF_DIM = 4 * D_MODEL
N_BRANCH = 2
N_GATE_ROWS = 4
F_BIAS_LO = 3.0
F_BIAS_HI = 6.0
EPS = 1e-6
IN_SIZES = (2 * MLSTM_DIM, MLSTM_DIM, MLSTM_DIM, SGU_DIM, SGU_DIM, N_BRANCH * D_MODEL, N_GATE_ROWS * MLSTM_HEADS)
D_IN = 2 * MLSTM_DIM + MLSTM_DIM + MLSTM_DIM + SGU_DIM + SGU_DIM + N_BRANCH * D_MODEL + N_GATE_ROWS * MLSTM_HEADS

kernel_name = 'hybrid_mlstm_sgu_block'


def rmsnorm(x, g):
    xf = x.astype(jnp.float32)
    y = xf * lax.rsqrt(jnp.mean(xf * xf, axis=-1, keepdims=True) + EPS)
    return (y * g.astype(jnp.float32)).astype(x.dtype)


def layernorm(x, g, b):
    xf = x.astype(jnp.float32)
    mu = jnp.mean(xf, axis=-1, keepdims=True)
    var = jnp.mean(jnp.square(xf - mu), axis=-1, keepdims=True)
    y = (xf - mu) * lax.rsqrt(var + EPS) * g.astype(jnp.float32) + b.astype(jnp.float32)
    return y.astype(x.dtype)


def modulate(h, shift, scale):
    return h * (1 + scale[:, None, :]) + shift[:, None, :]


def centred_conv(x, w, b):
    pad = CONV_WIDTH // 2
    S = x.shape[1]
    xp = jnp.pad(x, ((0, 0), (pad, pad), (0, 0)))
    y = b
    for j in range(CONV_WIDTH):
        y = y + xp[:, j:j + S, :] * w[j]
    return y


def mlstm_scan(q, k, v, li, lf):
    B, H, S, dk = q.shape
    dv = v.shape[-1]
    L = MLSTM_CHUNK
    NC = S // L

    def chunks(a):
        return jnp.moveaxis(a.reshape((B, H, NC, L) + a.shape[3:]), 2, 0)

    mask = jnp.tril(jnp.ones((L, L), dtype=bool))

    def step(carry, xs):
        C, n, m = carry
        qc, kc, vc, lic, lfc = xs
        b = jnp.cumsum(lfc, axis=-1)
        bL = b[..., -1]
        Dm = b[..., :, None] - b[..., None, :] + lic[..., None, :]
        Dm = jnp.where(mask, Dm, -jnp.inf)
        inter = b + m[..., None]
        m_t = jnp.maximum(inter, jnp.max(Dm, axis=-1))
        w_inter = jnp.exp(inter - m_t)
        P = jnp.exp(Dm - m_t[..., None]) * jnp.einsum('bhtd,bhsd->bhts', qc, kc)
        num = w_inter[..., None] * jnp.einsum('bhtd,bhde->bhte', qc, C) + jnp.einsum('bhts,bhse->bhte', P, vc)
        den = w_inter * jnp.einsum('bhtd,bhd->bht', qc, n) + jnp.sum(P, axis=-1)
        h = num / jnp.maximum(jnp.abs(den), jnp.exp(-m_t))[..., None]
        g = bL[..., None] - b + lic
        m_new = jnp.maximum(bL + m, jnp.max(g, axis=-1))
        wk = jnp.exp(g - m_new[..., None])[..., None] * kc
        decay = jnp.exp(bL + m - m_new)
        C_new = decay[..., None, None] * C + jnp.einsum('bhsd,bhse->bhde', wk, vc)
        n_new = decay[..., None] * n + jnp.sum(wk, axis=-2)
        return (C_new, n_new, m_new), h

    init = (jnp.zeros((B, H, dk, dv), jnp.float32),
            jnp.zeros((B, H, dk), jnp.float32),
            jnp.zeros((B, H), jnp.float32))
    _, hs = lax.scan(step, init, (chunks(q), chunks(k), chunks(v), chunks(li), chunks(lf)))
    return jnp.moveaxis(hs, 0, 2).reshape(B, H, S, dv)


def mlstm_bidirectional(q, k, v, gates):
    fwd = mlstm_scan(q, k, v, gates[0], jax.nn.log_sigmoid(gates[1]))
    flip = lambda a: jnp.flip(a, axis=2)
    bwd = mlstm_scan(flip(q), flip(k), flip(v), flip(gates[2]), flip(jax.nn.log_sigmoid(gates[3])))
    return fwd + flip(bwd)


def spatial_gating(u, v, ln_g, ln_b, w_s, b_s):
    B, S, _ = v.shape
    NC = S // SGU_CHUNK
    vn = layernorm(v, ln_g, ln_b).reshape(B, NC, SGU_CHUNK, SGU_GROUPS, SGU_GROUP_DIM)
    s = jnp.einsum('gpq,bnqgc->bnpgc', w_s, vn) + b_s.T[None, None, :, :, None]
    return u * s.reshape(B, S, SGU_DIM)


def hybrid_layer(x, mod, norm1_g, norm2_g, w_in, b_if, conv_w, conv_b, mh_g,
                 ln_v_g, ln_v_b, w_s, b_s, w_out, w1, w2):
    B, S, _ = x.shape
    H, dh = MLSTM_HEADS, MLSTM_HEAD_DIM
    sh1, sc1, g1, sh2, sc2, g2 = jnp.split(mod, 6, axis=-1)

    h = modulate(rmsnorm(x, norm1_g), sh1, sc1)
    z = h @ w_in
    cuts = []
    acc = 0
    for sz in IN_SIZES[:-1]:
        acc += sz
        cuts.append(acc)
    qk, v_a, o_a, u_b, v_b, mg_pre, if_pre = jnp.split(z, cuts, axis=-1)

    qk = jax.nn.silu(centred_conv(qk, conv_w, conv_b))
    q_a, k_a = jnp.split(qk, 2, axis=-1)

    def heads(a):
        return a.reshape(B, S, H, dh).transpose(0, 2, 1, 3).astype(jnp.float32)

    qh = heads(q_a) * (dh ** -0.5)
    kh = heads(k_a)
    vh = heads(v_a)
    gates = (if_pre.reshape(B, S, N_GATE_ROWS, H) + b_if).transpose(2, 0, 3, 1).astype(jnp.float32)
    hA = mlstm_bidirectional(qh, kh, vh, gates)
    hA = hA * lax.rsqrt(jnp.mean(hA * hA, axis=-1, keepdims=True) + EPS)
    hA = hA.transpose(0, 2, 1, 3).reshape(B, S, MLSTM_DIM) * mh_g.astype(jnp.float32)
    y_a = hA.astype(x.dtype) * jax.nn.sigmoid(o_a)

    y_b = spatial_gating(jax.nn.gelu(u_b, approximate=False), jax.nn.gelu(v_b, approximate=False),
                         ln_v_g, ln_v_b, w_s, b_s)

    gate_a, gate_b = jnp.split(jax.nn.sigmoid(mg_pre), 2, axis=-1)
    mix = gate_a * y_a + gate_b * y_b
    x = x + g1[:, None, :] * (mix @ w_out)

    h2 = modulate(rmsnorm(x, norm2_g), sh2, sc2)
    x = x + g2[:, None, :] * (jnp.square(jax.nn.relu(h2 @ w1)) @ w2)
    return x


def setup_inputs(seed: int = 0) -> dict:
    key = jax.random.key(seed)
    ks = jax.random.split(key, 20)
    D = D_MODEL
    H = MLSTM_HEADS
    nrm = jax.random.normal
    f32 = jnp.float32
    x = nrm(ks[0], (BATCH, SEQ, D), f32)
    c = nrm(ks[1], (BATCH, D), f32)
    w_ada = nrm(ks[2], (DEPTH, D, 6 * D), f32) * (0.5 * D ** -0.5)
    b_ada = 0.01 * nrm(ks[3], (DEPTH, 6 * D), f32)
    norm1_g = 1.0 + 0.01 * nrm(ks[4], (DEPTH, D), f32)
    norm2_g = 1.0 + 0.01 * nrm(ks[5], (DEPTH, D), f32)
    w_in = nrm(ks[6], (DEPTH, D, D_IN), f32) * D ** -0.5
    f_lin = jnp.linspace(F_BIAS_LO, F_BIAS_HI, H, dtype=f32)
    zeros_h = jnp.zeros((H,), f32)
    b_if = jnp.stack([zeros_h, f_lin, zeros_h, f_lin])[None] + 0.1 * nrm(ks[7], (DEPTH, N_GATE_ROWS, H), f32)
    conv_w = nrm(ks[8], (DEPTH, CONV_WIDTH, 2 * MLSTM_DIM), f32) * CONV_WIDTH ** -0.5
    conv_b = 0.01 * nrm(ks[9], (DEPTH, 2 * MLSTM_DIM), f32)
    mh_g = 1.0 + 0.01 * nrm(ks[10], (DEPTH, MLSTM_DIM), f32)
    ln_v_g = 1.0 + 0.01 * nrm(ks[11], (DEPTH, SGU_DIM), f32)
    ln_v_b = 0.01 * nrm(ks[12], (DEPTH, SGU_DIM), f32)
    w_s = nrm(ks[13], (DEPTH, SGU_GROUPS, SGU_CHUNK, SGU_CHUNK), f32) * SGU_CHUNK ** -0.5
    b_s = 1.0 + 0.1 * nrm(ks[14], (DEPTH, SGU_GROUPS, SGU_CHUNK), f32)
    w_out = nrm(ks[15], (DEPTH, D, D), f32) * D ** -0.5
    w1 = nrm(ks[16], (DEPTH, D, FF_DIM), f32) * D ** -0.5
    w2 = nrm(ks[17], (DEPTH, FF_DIM, D), f32) * FF_DIM ** -0.5
    normf_g = 1.0 + 0.01 * nrm(ks[18], (D,), f32)
    return {'x': x, 'c': c, 'w_ada': w_ada, 'b_ada': b_ada, 'norm1_g': norm1_g, 'norm2_g': norm2_g,
            'w_in': w_in, 'b_if': b_if, 'conv_w': conv_w, 'conv_b': conv_b, 'mh_g': mh_g,
            'ln_v_g': ln_v_g, 'ln_v_b': ln_v_b, 'w_s': w_s, 'b_s': b_s, 'w_out': w_out,
            'w1': w1, 'w2': w2, 'normf_g': normf_g}


def reference(x, c, w_ada, b_ada, norm1_g, norm2_g, w_in, b_if, conv_w, conv_b, mh_g,
              ln_v_g, ln_v_b, w_s, b_s, w_out, w1, w2, normf_g):
    c_act = jax.nn.silu(c)
    for l in range(DEPTH):
        mod = c_act @ w_ada[l] + b_ada[l]
        x = hybrid_layer(x, mod, norm1_g[l], norm2_g[l], w_in[l], b_if[l], conv_w[l], conv_b[l],
                         mh_g[l], ln_v_g[l], ln_v_b[l], w_s[l], b_s[l], w_out[l], w1[l], w2[l])
    return rmsnorm(x, normf_g)
```

```python
import numpy as np
import ml_dtypes
from contextlib import ExitStack
import concourse.bass as bass
import concourse.mybir as mybir
from concourse.bass_utils import run_bass_kernel_spmd

F32 = mybir.dt.float32
BF16 = mybir.dt.bfloat16
ALU = mybir.AluOpType
AF = mybir.ActivationFunctionType

D = 1024
S = 4096
NT = 32
DIN = 8208
EPS = 1e-6
ARENA_WORDS = 52992
ENGS = ["pe", "act", "dve", "pool", "sp"]


class Res:
    __slots__ = ("name", "lw", "rd")

    def __init__(self, name):
        self.name = name
        self.lw = None
        self.rd = []


class DSem:
    __slots__ = ("handle", "n", "last")

    def __init__(self, handle):
        self.handle = handle
        self.n = 0
        self.last = None


class Op:
    __slots__ = ("eng", "fn", "deps", "gid", "seg", "signal", "dsem", "dval", "count", "occ", "lat",
                 "pos", "fin", "nrem", "succ", "rdy", "waits", "is_bar", "line", "soft", "_excl")


SYNC_LAT = 0.35


class Sched:
    def __init__(self):
        self.all = []
        self.res = {}
        self.seg = 0

    def R(self, *key):
        r = self.res.get(key)
        if r is None:
            r = Res(key)
            self.res[key] = r
        return r

    def op(self, eng, fn, reads=(), writes=(), dsem=None, occ=0.3, lat=None):
        o = Op()
        o.eng = eng
        o.fn = fn
        o.gid = len(self.all)
        o.seg = self.seg
        o.signal = False
        o.dsem = dsem
        o.dval = 0
        o.count = 0
        o.occ = occ
        o.lat = occ if lat is None else lat
        o.is_bar = False
        import sys as _sys
        f_ = _sys._getframe(1)
        while f_.f_code.co_name != "build_program" and f_.f_back is not None and f_.f_code.co_name not in ("sweep_tile", "p2_proj", "ffn1"):
            f_ = f_.f_back
        o.line = f_.f_lineno
        deps = {}
        soft = set()
        hard = set()

        def add(d, is_soft=False):
            if d is not None and d.seg == o.seg:
                deps[d.gid] = d
                (soft if is_soft else hard).add(d.gid)

        excl = [r for r in reads if r.name[0] == "bank" and r not in writes]
        if excl:
            reads = [r for r in reads if r.name[0] != "bank"]
        for r in reads:
            add(r.lw)
        for w in writes:
            add(w.lw)
            for rr in w.rd:
                add(rr)
        for w in excl:
            lw = w.lw
            if lw is not None:
                add(lw, is_soft=(lw.eng == eng and w in getattr(lw, "_excl", ())))
            for rr in w.rd:
                add(rr)
        o.soft = soft - hard
        o._excl = excl
        writes = list(writes) + excl
        if dsem is not None:
            add(dsem.last)
            dsem.n += 1
            o.dval = 16 * dsem.n
            dsem.last = o
        o.deps = list(deps.values())
        for r in reads:
            if r.rd and r.rd[-1].seg != o.seg:
                r.rd = []
            r.rd.append(o)
        for w in writes:
            w.lw = o
            w.rd = []
        self.all.append(o)
        return o

    def barrier(self):
        self.seg += 1

    def schedule(self):
        import heapq
        nseg = self.seg + 1
        segs = [[] for _ in range(nseg)]
        for o in self.all:
            segs[o.seg].append(o)
        order = {e: [] for e in ENGS}
        for si, ops in enumerate(segs):
            for o in ops:
                o.succ = []
                o.nrem = len(o.deps)
                o.rdy = 0.0
            for o in ops:
                for d in o.deps:
                    d.succ.append(o)
            ready = {e: [] for e in ENGS}
            for o in ops:
                if o.nrem == 0:
                    heapq.heappush(ready[o.eng], (0.0, o.gid, o))
            free = {e: 0.0 for e in ENGS}
            seg_order = {e: [] for e in ENGS}
            left = len(ops)
            while left:
                best = None
                for e in ENGS:
                    if ready[e]:
                        rt, gid, o = ready[e][0]
                        st = max(rt, free[e])
                        if best is None or (st, gid) < (best[0], best[1]):
                            best = (st, gid, o)
                st, gid, o = best
                heapq.heappop(ready[o.eng])
                free[o.eng] = st + o.occ
                o.fin = st + o.lat
                seg_order[o.eng].append(o)
                left -= 1
                for sc in o.succ:
                    lat = 0.0 if (sc.eng == o.eng and o.dsem is None) else SYNC_LAT
                    sc.rdy = max(sc.rdy, o.fin + lat)
                    sc.nrem -= 1
                    if sc.nrem == 0:
                        heapq.heappush(ready[sc.eng], (sc.rdy, sc.gid, sc))
            last_dma = {}
            for o in ops:
                if o.dsem is not None:
                    last_dma[o.dsem] = o
            bar = Op()
            bar.eng = "sp"; bar.fn = None; bar.gid = -1; bar.seg = si; bar.signal = False
            bar.dsem = None; bar.dval = 0; bar.count = 0; bar.is_bar = True
            bar.deps = [seg_order[e][-1] for e in ENGS if e != "sp" and seg_order[e]] + list(last_dma.values())
            seg_order["sp"].append(bar)
            if si + 1 < nseg:
                for e in ENGS:
                    if e != "sp":
                        b2 = Op()
                        b2.eng = e; b2.fn = None; b2.gid = -1; b2.seg = si; b2.signal = False
                        b2.dsem = None; b2.dval = 0; b2.count = 0; b2.is_bar = True
                        b2.deps = [bar]
                        seg_order[e].append(b2)
            for e in ENGS:
                order[e] += seg_order[e]
        for e in ENGS:
            for p, o in enumerate(order[e]):
                o.pos = p
        for e in ENGS:
            seen = {}
            for o in order[e]:
                need = {}
                for d in o.deps:
                    if d.dsem is not None:
                        key, val = d.dsem, d.dval
                    else:
                        if d.eng == e and (e == "pe" or e == "sp" or d.gid in getattr(o, "soft", ())):
                            continue
                        key, val = d.eng, d.pos
                    if seen.get(key, -1) >= val:
                        continue
                    if key not in need or need[key][0] < val:
                        need[key] = (val, d)
                o.waits = []
                for key, (val, d) in need.items():
                    seen[key] = val
                    if d.dsem is None:
                        d.signal = True
                    o.waits.append(d)
        for e in ENGS:
            c = 0
            for o in order[e]:
                if o.signal and o.dsem is None:
                    c += 1
                    o.count = c
        return order


def build_program(debug=False, stop_after=99):
    nc = bass.Bass("TRN2", target_bir_lowering=False)
    SCH = Sched()
    R = SCH.R

    def din(name, shape, dt=F32):
        return nc.dram_tensor(name, list(shape), dt, kind="ExternalInput").ap()

    x = din("x", [S, D])
    c_b = din("c_b", [128, 8])
    w_ada = din("w_ada", [D, 6 * D])
    b_ada = din("b_ada", [1, 6 * D])
    n1gb = din("n1gb", [128, D])
    n2gp = din("n2gp", [128, 8])
    w_in = din("w_in", [D, DIN])
    bifb = din("bifb", [128, 512])
    cwp = din("cwp", [128, 48])
    cbp = din("cbp", [128, 16])
    mhgb = din("mhgb", [128, D])
    lngb = din("lngb", [128, D])
    lnbb = din("lnbb", [128, D])
    wsT = din("wsT", [128, 1024])
    bsp = din("bsp", [128, 8])
    w_out = din("w_out", [D, D])
    w1 = din("w1", [D, 4 * D])
    w2 = din("w2", [4 * D, D])
    nfgb = din("nfgb", [128, D])
    cmask = din("cmask", [128, 640])
    identb = din("identb", [128, 128], BF16)
    out = nc.dram_tensor("out", [S, D], F32, kind="ExternalOutput").ap()
    sk = "ExternalOutput" if debug else "Internal"
    mod_d = nc.dram_tensor("mod_d", [6 * D], F32, kind=sk).ap()
    mix_d = nc.dram_tensor("mix_d", [S, D], BF16, kind=sk).ap()
    hf_d = nc.dram_tensor("hf_d", [S, 256], F32, kind=sk).ap()
    mixa_d = nc.dram_tensor("mixa_d", [S, D], BF16, kind=sk).ap()
    w1b_d = nc.dram_tensor("w1b_d", [D, 4 * D], BF16, kind="Internal").ap()
    w2b_d = nc.dram_tensor("w2b_d", [4 * D, D], BF16, kind="Internal").ap()
    x1_d = nc.dram_tensor("x1_d", [S, D], F32, kind=sk).ap()

    es = ExitStack()
    with es:
        arena = es.enter_context(nc.sbuf_tensor("arena", [128, ARENA_WORDS], F32))
        banks = [es.enter_context(nc.psum_tensor(f"ps{i}", [128, 512], F32)) for i in range(8)]
        esem = {e: es.enter_context(nc.semaphore(f"sem_{e}")) for e in ENGS}
        nsem = [0]

        def new_dsem():
            nsem[0] += 1
            return DSem(es.enter_context(nc.semaphore(f"dsem{nsem[0]}")))

        ptr = [0]

        def alloc(shape, dt=F32):
            n = 1
            for s_ in shape[1:]:
                n *= s_
            nb = n * (4 if dt == F32 else 2)
            nb = (nb + 31) // 32 * 32
            off = ptr[0]
            ptr[0] += nb
            assert ptr[0] <= max(top[0], off + nb) and ptr[0] <= ARENA_WORDS * 4, ("arena overflow", ptr[0], top[0])
            ap = arena[:, off // 4:(off + nb) // 4]
            if dt != F32:
                ap = ap.bitcast(dt)
            ap = ap[:, 0:n]
            if len(shape) == 3:
                ap = ap.rearrange("p (a b) -> p a b", a=shape[1], b=shape[2])
            return ap

        top = [ARENA_WORDS * 4]

        def alloc_top(shape, dt=F32):
            n = 1
            for s_ in shape[1:]:
                n *= s_
            nb = (n * (4 if dt == F32 else 2) + 31) // 32 * 32
            top[0] -= nb
            save = ptr[0]
            ptr[0] = top[0]
            ap = alloc(shape, dt)
            ptr[0] = save
            return ap

        def bank_bf(i):
            return banks[i][:, :].bitcast(BF16)

        def PB(i):
            return R("bank", i)

        def nfree(ap):
            n = 1
            for s_ in ap.shape[1:]:
                n *= s_
            return n

        def E(eng, fn, reads=(), writes=(), occ=0.3):
            return SCH.op(eng, fn, reads, writes, occ=occ)

        def DMA(q, o_ap, i_ap, reads, writes, sem, **kw):
            if q == "pool":
                sem = new_dsem()
            nbytes = nfree(o_ap) * o_ap.shape[0] * (4 if o_ap.dtype == F32 else 2)
            return SCH.op(q, lambda e, o_ap=o_ap, i_ap=i_ap, kw=kw: e.dma_start(out=o_ap, in_=i_ap, **kw),
                          reads, writes, dsem=sem, occ=0.4 if q == "sp" else 1.5, lat=3.0 + nbytes / 120e3)

        def MMG(items, reads, writes):
            def fn(e, items=items):
                ins = None
                for (o_, l_, r_, st, sp) in items:
                    ins = e.matmul(o_, l_, r_, start=st, stop=sp)
                return ins
            t = 0.0
            for (o_, l_, r_, st, sp) in items:
                t += max(64, nfree(r_)) * 0.00043 * (4 if l_.dtype == F32 else 1) + 0.01
            return SCH.op("pe", fn, reads, writes, occ=t, lat=t + 0.15)

        def TRG(items, reads, writes):
            def fn(e, items=items):
                ins = None
                for (o_, i_) in items:
                    ins = e.transpose(o_, i_, ident)
                return ins
            t = 0.11 * len(items)
            return SCH.op("pe", fn, reads, writes, occ=t, lat=t + 0.15)

        def ACTF(o_, i_, func, reads, writes, eng="act", **kw):
            t = 0.22 + nfree(o_) * 0.00085 + (0.1 if "accum_out" in kw else 0.0)
            return SCH.op("act", lambda e, o_=o_, i_=i_, func=func, kw=kw: e.activation(o_, i_, func, **kw),
                          reads, writes, occ=t)

        def vcost(eng, o_):
            return (0.12 + nfree(o_) * 0.00104) if eng == "dve" else (0.15 + nfree(o_) * 0.0022)

        def TS(eng, o_, i_, s1, s2, op0, op1, reads, writes):
            if op1 is None:
                return E(eng, lambda e, o_=o_, i_=i_: e.tensor_scalar(o_, i_, s1, None, op0), reads, writes, vcost(eng, o_))
            return E(eng, lambda e, o_=o_, i_=i_: e.tensor_scalar(o_, i_, s1, s2, op0, op1), reads, writes, vcost(eng, o_))

        def TT(eng, o_, a_, b_, op, reads, writes):
            return E(eng, lambda e, o_=o_, a_=a_, b_=b_, op=op: e.tensor_tensor(o_, a_, b_, op), reads, writes,
                     vcost(eng, o_))

        def STT(o_, a_, sc, b_, op0, op1, reads, writes):
            return E("dve", lambda e, o_=o_, a_=a_, sc=sc, b_=b_, op0=op0, op1=op1:
                     e.scalar_tensor_tensor(o_, a_, sc, b_, op0, op1), reads, writes, vcost("dve", o_))

        def RSTD(ve_ap, out_ap, r_in, r_out):
            return TT("pool", out_ap, ve_ap, neghalf[:, 0:1], ALU.pow, [r_in, RC], [r_out])

        csem = new_dsem()
        RC = R("consts")
        ident = alloc([128, 128], BF16)
        cm = alloc([128, 640])
        maskF = cm[:, 0:128]
        maskB = cm[:, 128:256]
        ones = cm[:, 256:384]
        maskFq = cm[:, 384:512]
        maskBq = cm[:, 512:640]
        neghalf = alloc([128, 8])
        cB = alloc([128, 8])
        bifs = alloc([128, 512])
        cws = alloc([128, 48])
        cbs = alloc([128, 16])
        bss = alloc([128, 8])
        n2gs = alloc([128, 8])
        for (dst, src) in [(ident, identb), (cm, cmask), (cB, c_b), (bifs, bifb), (cws, cwp), (cbs, cbp),
                           (bss, bsp), (n2gs, n2gp)]:
            DMA("sp", dst, src, [], [RC], csem)
        E("pool", lambda e: e.memset(neghalf, -0.5), [], [RC])
        hT = alloc([128, 8, S], BF16)
        base_persist = ptr[0]

        def RH(i):
            return R("hT", i)

        cact = alloc([128, 8])
        csig = alloc([128, 8])
        bada_s = alloc([1, 6 * D])
        rowb = [alloc([1, 512]) for _ in range(2)]
        wst = [alloc([128, 8, 512]) for _ in range(2)]
        wst_sem = [new_dsem() for _ in range(2)]
        row_sem = [new_dsem() for _ in range(2)]
        DMA("sp", bada_s[0:1, :], b_ada[0:1, :], [], [RC], csem)
        ACTF(cact, cB, AF.Silu, [RC], [R("cact")])
        RMOD = R("mod_d")
        for nb in range(12):
            b = nb % 2
            DMA("sp", wst[b], w_ada[:, nb * 512:(nb + 1) * 512].rearrange("(kc p) n -> p kc n", p=128),
                [], [R("wst", b)], wst_sem[b])
            MMG([(banks[b][0:1, 0:512], cact[:, kc:kc + 1], wst[b][:, kc, :], kc == 0, kc == 7) for kc in range(8)],
                [R("cact"), R("wst", b)], [PB(b)])
            TT("dve", rowb[b][0:1, :], banks[b][0:1, 0:512], bada_s[0:1, nb * 512:(nb + 1) * 512], ALU.add,
               [PB(b), RC], [R("rowb", b)])
            DMA("sp", mod_d[nb * 512:(nb + 1) * 512].rearrange("(o n) -> o n", o=1), rowb[b][0:1, :],
                [R("rowb", b)], [RMOD], row_sem[b])
        SCH.barrier()
        ptr[0] = base_persist

        wsgu = alloc([128, 8, 3072], BF16)
        wsgu_sem = new_dsem()
        RW = R("lnc")
        RWs = [R("wsgu", 0), R("wsgu", 1)]
        wss = alloc([128, 8, 128], BF16)

        def p2_weight_loads(after):
            DMA("pool", wsgu[:, :, 0:2048], w_in[:, 4096:6144].rearrange("(kc p) n -> p kc n", p=128), after, [RWs[0]], None)
            DMA("pool", wsgu[:, :, 2048:3072], w_in[:, 7168:8192].rearrange("(kc p) n -> p kc n", p=128), after, [RWs[1]], None)
            DMA("pool", wss, wsT.rearrange("p (g q) -> p g q", g=8), after, [R("wss")], None)
        lgs = alloc([128, D])
        lbs = alloc([128, D])
        DMA("sp", lgs, lngb, [], [RW], wsgu_sem)
        DMA("sp", lbs, lnbb, [], [RW], wsgu_sem)
        base_p2w = ptr[0]
        A1b = alloc([128, D])
        B1b = alloc([128, D])
        n1s = alloc([128, D])
        p1sem = new_dsem()
        DMA("sp", n1s, n1gb, [], [R("p1c")], p1sem)
        DMA("sp", A1b, mod_d[1024:2048].partition_broadcast(128), [RMOD], [R("p1c")], p1sem)
        DMA("sp", B1b, mod_d[0:1024].partition_broadcast(128), [RMOD], [R("p1c")], p1sem)
        STT(A1b, A1b, 1.0, n1s, ALU.add, ALU.mult, [R("p1c")], [R("A1b")])
        NXB = 6
        xb = [alloc([128, D]) for _ in range(NXB)]
        xsem = [new_dsem() for _ in range(NXB)]
        junk = alloc([128, D], BF16)
        ss = alloc([128, 64])
        t1 = [alloc([128, D]) for _ in range(2)]
        xh = [alloc([128, D], BF16) for _ in range(2)]
        for i in range(NT):
            k3, k2 = i % NXB, i % 2
            DMA("sp", xb[k3], x[i * 128:(i + 1) * 128, :], [], [R("xb", k3)], xsem[k3])
            ACTF(junk, xb[k3], AF.Square, [R("xb", k3)], [R("junk"), R("ss", i)], accum_out=ss[:, i:i + 1])
            if i == 5:
                p2_weight_loads([R("ss", 5)])
            TS("dve", ss[:, 32 + i:33 + i], ss[:, i:i + 1], 1.0 / D, EPS, ALU.mult, ALU.add, [R("ss", i)], [R("ve", i)])
            RSTD(ss[:, 32 + i:33 + i], ss[:, i:i + 1], R("ve", i), R("rs", i))
            STT(t1[k2], xb[k3], ss[:, i:i + 1], A1b, ALU.mult, ALU.mult, [R("xb", k3), R("rs", i), R("A1b")], [R("t1", k2)])
            TT("dve", xh[k2], t1[k2], B1b, ALU.add, [R("t1", k2), R("p1c")], [R("xh", k2)])
            TRG([(bank_bf(k2)[:, kc * 128:(kc + 1) * 128], xh[k2][:, kc * 128:(kc + 1) * 128]) for kc in range(8)],
                [R("xh", k2), RC], [PB(k2)])
            ACTF(hT[:, :, i * 128:(i + 1) * 128], bank_bf(k2).rearrange("p (a b) -> p a b", a=8, b=128), AF.Copy,
                 [PB(k2)], [RH(i)])
        SCH.barrier()
        ptr[0] = base_persist
        if stop_after <= 1:
            return finish(nc, SCH, esem, None)

        ptr[0] = base_p2w
        wg = alloc_top([128, 8, 16], BF16)
        mhs = alloc_top([128, D])
        wqk_top = alloc_top([128, 8, 512], BF16)
        wv_top = alloc_top([128, 8, 256], BF16)
        p3sem = new_dsem()
        RP3 = R("p3c")
        DMA("pool", wg, w_in[:, 8192:8208].rearrange("(kc p) n -> p kc n", p=128), [], [R("wg")], None)
        DMA("sp", mhs, mhgb, [], [RP3], p3sem)
        for wi, (dst, c0) in enumerate([(wqk_top[:, :, 0:256], 0), (wqk_top[:, :, 256:512], 1024), (wv_top, 2048)]):
            DMA("pool", dst, w_in[:, c0:c0 + 256].rearrange("(kc p) n -> p kc n", p=128), [], [R("wh", 0, wi)], None)
        sg = [alloc([128, D]) for _ in range(2)]
        gu = [alloc([128, D]) for _ in range(2)]
        gv = [alloc([128, D]) for _ in range(2)]
        tln = alloc([128, D])
        vn = [alloc([128, D], BF16) for _ in range(2)]
        yb = alloc([128, D])
        mxb = [alloc([128, D], BF16) for _ in range(2)]
        mxb_sem = [new_dsem() for _ in range(2)]
        st2 = alloc([128, 32 * 16])

        def RM(i):
            return R("mix_d", i)

        def p2_proj(i):
            for g in range(6):
                MMG([(banks[g][:, 0:512], hT[:, kc, i * 128:(i + 1) * 128], wsgu[:, kc, g * 512:(g + 1) * 512],
                      kc == 0, kc == 7) for kc in range(8)], [RH(i)] + RWs, [PB(g)])

        p2_proj(0)
        for i in range(NT):
            k = i % 2
            sb = i * 16
            def act_u():
                for hh in range(2):
                    cs = slice(hh * 512, (hh + 1) * 512)
                    ACTF(gu[k][:, cs], banks[hh][:, :], AF.Gelu, [PB(hh)], [R("gu", k, hh)])

            def act_v():
                for hh in range(2):
                    cs = slice(hh * 512, (hh + 1) * 512)
                    ACTF(gv[k][:, cs], banks[2 + hh][:, :], AF.Gelu, [PB(2 + hh)], [R("gv", k, hh)])
                    E("dve", lambda e, o_=st2[:, sb + hh * 6:sb + hh * 6 + 6], i_=gv[k][:, cs]: e.bn_stats(o_, i_),
                      [R("gv", k, hh)], [R("st2", i, hh)])

            def act_m():
                for hh in range(2):
                    cs = slice(hh * 512, (hh + 1) * 512)
                    ACTF(sg[k][:, cs], banks[4 + hh][:, :], AF.Sigmoid, [PB(4 + hh)], [R("sg", k, hh)])

            if i % 2 == 0:
                act_u(); act_v(); act_m()
            else:
                act_m(); act_v(); act_u()
            if i + 1 < NT:
                p2_proj(i + 1)
            E("dve", lambda e, o_=st2[:, sb + 12:sb + 14], i_=st2[:, sb:sb + 12]: e.bn_aggr(o_, i_),
              [R("st2", i, 0), R("st2", i, 1)], [R("mv", i)])
            TS("dve", st2[:, sb + 14:sb + 15], st2[:, sb + 13:sb + 14], EPS, None, ALU.add, None,
               [R("mv", i)], [R("ve2", i)])
            RSTD(st2[:, sb + 14:sb + 15], st2[:, sb + 15:sb + 16], R("ve2", i), R("rs2", i))
            STT(tln, gv[k], st2[:, sb + 12:sb + 13], lgs, ALU.subtract, ALU.mult,
                [R("gv", k, 0), R("gv", k, 1), R("mv", i), RW], [R("tln")])
            STT(vn[k], tln, st2[:, sb + 15:sb + 16], lbs, ALU.mult, ALU.add, [R("tln"), R("rs2", i), RW], [R("vn", k)])
            for g8 in range(8):
                bk = 6 + g8 // 4
                MMG([(banks[bk][:, (g8 % 4) * 128:(g8 % 4 + 1) * 128], wss[:, g8, :], vn[k][:, g8 * 128:(g8 + 1) * 128],
                      True, True)], [R("vn", k), R("wss")], [PB(bk)])
            for g8 in range(8):
                bk = 6 + g8 // 4
                gs = slice(g8 * 128, (g8 + 1) * 128)
                STT(yb[:, gs], banks[bk][:, (g8 % 4) * 128:(g8 % 4 + 1) * 128], bss[:, g8:g8 + 1], gu[k][:, gs],
                    ALU.add, ALU.mult, [PB(bk), RC, R("gu", k, g8 // 4)], [R("yb", g8)])
            TT("pool", mxb[k], yb, sg[k], ALU.mult,
               [R("yb", g8) for g8 in range(8)] + [R("sg", k, 0), R("sg", k, 1)], [R("mxb", k)])
            DMA("sp", mix_d[i * 128:(i + 1) * 128, :], mxb[k], [R("mxb", k)], [RM(i)], mxb_sem[k])
        SCH.barrier()
        ptr[0] = base_persist
        if stop_after <= 2:
            return finish(nc, SCH, esem, None)

        G = alloc([128, 32, 16])
        SPf = alloc([128, 128])
        SPb = alloc([128, 128])
        tmpe = alloc([128, 128])
        gsc = {nm: alloc([128, 128]) for nm in ["wF", "uF", "sF", "wB", "uB", "sB", "sF16", "sB16"]}
        RG = R("gates")
        for i in range(NT):
            MMG([(banks[0][:, i * 16:(i + 1) * 16], hT[:, kc, i * 128:(i + 1) * 128], wg[:, kc, :], kc == 0, kc == 7)
                 for kc in range(8)], [RH(i), R("wg")], [PB(0)])
        TT("dve", G.rearrange("p a b -> p (a b)"), banks[0][:, :], bifs, ALU.add, [PB(0), RC], [RG])
        for (SPx, c0, key) in [(SPf, 4, "SPf"), (SPb, 12, "SPb")]:
            ACTF(tmpe.rearrange("p (a b) -> p a b", a=32, b=4), G[:, :, c0:c0 + 4], AF.Exp, [RG], [R("tmpe")], scale=-1.0)
            ACTF(SPx, tmpe, AF.Ln, [R("tmpe")], [R(key)], bias=1.0)
        MMG([(banks[1][:, 0:128], maskF, SPf, True, True)], [R("SPf"), RC], [PB(1)])
        MMG([(banks[1][:, 128:256], maskB, SPb, True, True)], [R("SPb"), RC], [PB(1)])
        MMG([(banks[1][:, 256:384], ones, SPf, True, True)], [R("SPf"), RC], [PB(1)])
        MMG([(banks[1][:, 384:512], ones, SPb, True, True)], [R("SPb"), RC], [PB(1)])
        RGS = R("gsc")
        ACTF(gsc["wF"], banks[1][:, 0:128], AF.Exp, [PB(1)], [RGS], scale=-1.0)
        ACTF(gsc["wB"], banks[1][:, 128:256], AF.Exp, [PB(1)], [RGS], scale=-1.0)
        ACTF(gsc["sF"], banks[1][:, 256:384], AF.Exp, [PB(1)], [RGS], scale=-1.0)
        ACTF(gsc["sB"], banks[1][:, 384:512], AF.Exp, [PB(1)], [RGS], scale=-1.0)
        TS("dve", gsc["sF16"], gsc["sF"], 1.0 / 16.0, None, ALU.mult, None, [RGS], [R("gs16", 0)])
        TS("dve", gsc["sB16"], gsc["sB"], 1.0 / 16.0, None, ALU.mult, None, [RGS], [R("gs16", 1)])
        TT("dve", tmpe.rearrange("p (a b) -> p a b", a=32, b=4), banks[1][:, 0:128].rearrange("p (a b) -> p a b", a=32, b=4),
           G[:, :, 0:4], ALU.add, [PB(1), RG, R("SPf"), R("SPb")], [R("tmpe")])
        ACTF(gsc["uF"], tmpe, AF.Exp, [R("tmpe")], [RGS])
        TT("dve", tmpe.rearrange("p (a b) -> p a b", a=32, b=4), banks[1][:, 128:256].rearrange("p (a b) -> p a b", a=32, b=4),
           G[:, :, 8:12], ALU.add, [PB(1), RG, RGS], [R("tmpe")])
        ACTF(gsc["uB"], tmpe, AF.Exp, [R("tmpe")], [RGS])

        SCH.barrier()
        if stop_after <= 2.2:
            return finish(nc, SCH, esem, None)
        qT = alloc([128, 2, S], BF16)
        kT = alloc([128, 2, S], BF16)
        vext = alloc([128, 32, 258], BF16)
        E("pool", lambda e: e.memset(vext[:, :, 256:258], 1.0), [], [R("vones")])
        wqk2 = [wqk_top, alloc([128, 8, 512], BF16)]
        wv2 = [wv_top, alloc([128, 8, 256], BF16)]
        wog = alloc([128, 8, 512], BF16)
        Chat = {d_: alloc([128, 2, 257]) for d_ in "FB"}
        Cbf = {d_: alloc([128, 2, 258], BF16) for d_ in "FB"}
        base_p3 = ptr[0]
        zst = alloc([128, 4098])
        acc = alloc([128, 4096])
        end_conv = ptr[0]
        ptr[0] = base_p3
        PT = [alloc([128, 128], BF16) for _ in range(2)]
        kw = [alloc([128, 256], BF16) for _ in range(2)]
        nds = [alloc([128, 257]) for _ in range(2)]
        sml = alloc([128, 64 * 4])
        hfo = [alloc([128, 256]) for _ in range(4)]
        hfo_sem = [new_dsem() for _ in range(4)]
        hfi = [alloc([128, 256]) for _ in range(4)]
        hfi_sem = [new_dsem() for _ in range(4)]
        mxo = [alloc([128, 256], BF16) for _ in range(4)]
        mxo_sem = [new_dsem() for _ in range(4)]
        sgo = [alloc([128, 512]) for _ in range(2)]
        hs = [alloc([128, 256]) for _ in range(2)]
        y1 = [alloc([128, 256]) for _ in range(2)]
        y2 = [alloc([128, 256]) for _ in range(2)]
        junk3 = alloc([128, 256], BF16)
        ptr[0] = max(ptr[0], end_conv)

        def RHF(i):
            return R("hf_d", i)

        for h in range(4):
            hc = slice(h * 256, (h + 1) * 256)
            wqk = wqk2[h % 2]
            wv = wv2[h % 2]

            def load_qkv(hh_):
                for wi, (dst, c0) in enumerate([(wqk2[hh_ % 2][:, :, 0:256], hh_ * 256),
                                                (wqk2[hh_ % 2][:, :, 256:512], 1024 + hh_ * 256),
                                                (wv2[hh_ % 2], 2048 + hh_ * 256)]):
                    DMA("pool", dst, w_in[:, c0:c0 + 256].rearrange("(kc p) n -> p kc n", p=128), [],
                        [R("wh", hh_ % 2, wi)], None)

            for wi, (dst, c0) in enumerate([(wog[:, :, 0:256], 3072 + h * 256), (wog[:, :, 256:512], 6144 + h * 256)]):
                DMA("pool", dst, w_in[:, c0:c0 + 256].rearrange("(kc p) n -> p kc n", p=128), [], [R("wog", wi)], None)
            RWQK = [R("wh", h % 2, 0), R("wh", h % 2, 1)]
            RWV = [R("wh", h % 2, 2)]
            RWOG = [R("wog", 0), R("wog", 1)]
            E("pool", lambda e: e.memset(zst[:, 0:1], 0.0), [], [R("zpad")])
            E("pool", lambda e: e.memset(zst[:, 4097:4098], 0.0), [], [R("zpad")])
            for ct in range(4):
                cti = (0 if ct < 2 else 8) + h * 2 + (ct % 2)
                for tb in range(8):
                    bk = tb % 2
                    MMG([(banks[bk][:, 0:512], wqk[:, kc, ct * 128:(ct + 1) * 128], hT[:, kc, tb * 512:(tb + 1) * 512],
                          kc == 0, kc == 7) for kc in range(8)], [RH(4 * tb + j) for j in range(4)] + RWQK, [PB(bk)])
                    if tb % 2 == 0:
                        ACTF(zst[:, 1 + tb * 512:1 + (tb + 1) * 512], banks[bk][:, :], AF.Copy, [PB(bk)], [R("z", tb)])
                    else:
                        E("dve", lambda e, o_=zst[:, 1 + tb * 512:1 + (tb + 1) * 512], i_=banks[bk][:, :]: e.tensor_copy(o_, i_),
                          [PB(bk)], [R("z", tb)], 0.65)
                for hf2 in range(2):
                    a0 = hf2 * 2048
                    zr = [R("z", tb) for tb in range(8)] + [R("zpad")]
                    ACTF(acc[:, a0:a0 + 2048], zst[:, 1 + a0:1 + a0 + 2048], AF.Identity, zr + [RC], [R("acc", hf2)],
                         scale=cws[:, cti * 3 + 1:cti * 3 + 2], bias=cbs[:, cti:cti + 1])
                    STT(acc[:, a0:a0 + 2048], zst[:, a0:a0 + 2048], cws[:, cti * 3:cti * 3 + 1], acc[:, a0:a0 + 2048],
                        ALU.mult, ALU.add, zr + [RC, R("acc", hf2)], [R("acc", hf2)])
                    STT(acc[:, a0:a0 + 2048], zst[:, 2 + a0:2 + a0 + 2048], cws[:, cti * 3 + 2:cti * 3 + 3], acc[:, a0:a0 + 2048],
                        ALU.mult, ALU.add, zr + [RC, R("acc", hf2)], [R("acc", hf2)])
                    dst = (qT if ct < 2 else kT)[:, ct % 2, a0:a0 + 2048]
                    rdst = [R("qk", ct, 16 * hf2 + j) for j in range(16)]
                    ACTF(dst, acc[:, a0:a0 + 2048], AF.Silu, [R("acc", hf2)], rdst)
            for i in range(NT):
                bk = 2 + i % 2
                MMG([(banks[bk][:, 0:256], hT[:, kc, i * 128:(i + 1) * 128], wv[:, kc, :], kc == 0, kc == 7)
                     for kc in range(8)], [RH(i)] + RWV, [PB(bk)])
                ACTF(vext[:, i, 0:256], banks[bk][:, 0:256], AF.Copy, [PB(bk)], [R("v", i)])
            SCH.barrier()
            if stop_after <= 2.4:
                return finish(nc, SCH, esem, None)

            def sweep_tile(i, dirn, first, prev_i, fin):
                par = 0 if dirn == "F" else 1
                P0, P1, P2, P3 = [4 * par + j for j in range(4)]
                col = i * 4 + h
                uX = gsc["u" + dirn][:, col:col + 1]
                wX = gsc["w" + dirn][:, col:col + 1]
                msk = maskFq if dirn == "F" else maskBq
                ts_ = slice(i * 128, (i + 1) * 128)
                rq = [R("qk", 0, i), R("qk", 1, i)]
                rk = [R("qk", 2, i), R("qk", 3, i)]
                RCh = R("Chat", dirn)
                RCb = R("Cbf", dirn)
                MMG([(banks[P0][:, 0:128], kT[:, j, ts_], qT[:, j, ts_], j == 0, j == 1) for j in range(2)],
                    rq + rk, [PB(P0)])
                TRG([(bank_bf(P0)[:, 256 + j * 128:256 + (j + 1) * 128], kT[:, j, ts_]) for j in range(2)],
                    rk + [RC], [PB(P0)])
                STT(PT[par], banks[P0][:, 0:128], uX, msk, ALU.mult, ALU.mult, [PB(P0), RGS, RC], [R("PT", par)])
                ACTF(kw[par], bank_bf(P0)[:, 256:512], AF.Copy, [PB(P0), RGS], [R("kw", par)], scale=uX)
                items = []
                if not first:
                    sp_ = gsc["s" + dirn][:, prev_i * 4 + h:prev_i * 4 + h + 1]
                    sp16 = gsc["s" + dirn + "16"][:, prev_i * 4 + h:prev_i * 4 + h + 1]
                    TS("pool", Cbf[dirn][:, 0, 0:257], Chat[dirn][:, 0, :], sp_, 1.0 / 16.0, ALU.mult, ALU.mult,
                       [RCh, R("Chn", dirn), RGS], [R("Cbf", dirn, 0)])
                    if fin:
                        ACTF(Cbf[dirn][:, 1, 0:257], Chat[dirn][:, 1, :], AF.Copy, [RCh, R("Chn", dirn), RGS, R("gs16", 0), R("gs16", 1)], [R("Cbf", dirn, 1)], scale=sp16)
                    else:
                        TS("pool", Cbf[dirn][:, 1, 0:257], Chat[dirn][:, 1, :], sp_, 1.0 / 16.0, ALU.mult, ALU.mult,
                           [RCh, R("Chn", dirn), RGS], [R("Cbf", dirn, 1)])
                    items += [(banks[P1][:, 0:257], qT[:, j, ts_], Cbf[dirn][:, j, 0:257], j == 0, False) for j in range(2)]
                items += [(banks[P1][:, 0:257], PT[par], vext[:, i, 0:257], first, True)]
                MMG(items, rq + [R("Cbf", dirn, 0), R("Cbf", dirn, 1), R("PT", par), R("v", i), R("vones")], [PB(P1)])
                MMG([(banks[P1][:, 260 + j:261 + j], kw[par][:, j * 128:(j + 1) * 128], vext[:, i, 256:257], True, True)
                     for j in range(2)], [R("kw", par), R("vones")], [PB(P1)])
                MMG([(banks[P2][:, j * 256:(j + 1) * 256], kw[par][:, j * 128:(j + 1) * 128], vext[:, i, 0:256], True, True)
                     for j in range(2)], [R("kw", par), R("v", i)], [PB(P2)])
                ACTF(nds[par], banks[P1][:, 0:257], AF.Copy, [PB(P1), RGS], [R("nds", par)], scale=wX)
                so = ((i % 16) * 2 + (0 if dirn == "F" else 1)) * 8
                dn = sml[:, so:so + 1]
                rd = sml[:, so + 1:so + 2]
                ACTF(dn, nds[par][:, 256:257], AF.Abs, [R("nds", par)], [R("dn", so)])
                TS("dve", rd, dn, 1.0, None, ALU.max, None, [R("dn", so)], [R("dn2", so)])
                E("dve", lambda e, o_=rd, i_=rd: e.reciprocal(o_, i_), [R("dn2", so)], [R("rd", so)], 0.2)
                dC3 = banks[P2][:, :].rearrange("p (a b) -> p a b", a=2, b=256)
                if first:
                    E("dve", lambda e, o_=Chat[dirn][:, :, 0:256], i_=dC3: e.tensor_copy(o_, i_), [PB(P2)], [RCh], 0.65)
                    E("dve", lambda e, o_=Chat[dirn][:, :, 256], i_=banks[P1][:, 260:262]: e.tensor_copy(o_, i_),
                      [PB(P1)], [R("Chn", dirn)], 0.15)
                else:
                    STT(Chat[dirn][:, :, 0:256], Chat[dirn][:, :, 0:256], sp_, dC3, ALU.mult, ALU.add,
                        [RCh, RGS, PB(P2)], [RCh])
                    STT(Chat[dirn][:, :, 256], Chat[dirn][:, :, 256], sp_, banks[P1][:, 260:262], ALU.mult, ALU.add,
                        [R("Chn", dirn), RGS, PB(P1)], [R("Chn", dirn)])
                if not fin:
                    k3 = 2 * par + i % 2
                    TS("dve", hfo[k3], nds[par][:, 0:256], rd, None, ALU.mult, None, [R("nds", par), R("rd", so)],
                       [R("hfo", k3)])
                    DMA("sp", hf_d[ts_, :], hfo[k3], [R("hfo", k3)], [RHF(i)], hfo_sem[k3])
                    return
                k3 = 2 * par + i % 2
                MMG([(banks[P3][:, 0:512], hT[:, kc, ts_], wog[:, kc, :], kc == 0, kc == 7) for kc in range(8)],
                    [RH(i)] + RWOG, [PB(P3)])
                ACTF(sgo[par], banks[P3][:, :], AF.Sigmoid, [PB(P3)], [R("sgo", par)])
                DMA("sp", hfi[k3], hf_d[ts_, :], [RHF(i)], [R("hfi", k3)], hfi_sem[k3])
                STT(hs[par], nds[par][:, 0:256], rd, hfi[k3], ALU.mult, ALU.add,
                    [R("nds", par), R("rd", so), R("hfi", k3)], [R("hs", par)])
                sq = sml[:, so + 2:so + 3]
                ve = sml[:, so + 3:so + 4]
                rs = sml[:, so + 4:so + 5]
                ACTF(junk3, hs[par], AF.Square, [R("hs", par)], [R("junk3"), R("sq", so)], accum_out=sq)
                TS("dve", ve, sq, 1.0 / 256.0, EPS, ALU.mult, ALU.add, [R("sq", so)], [R("ve3", so)])
                RSTD(ve, rs, R("ve3", so), R("rs3", so))
                TT("pool", y2[par], sgo[par][:, 0:256], sgo[par][:, 256:512], ALU.mult, [R("sgo", par)], [R("y2", par)])
                TT("pool", y2[par], y2[par], mhs[:, hc], ALU.mult, [R("y2", par), RP3], [R("y2", par)])
                STT(mxo[k3], hs[par], rs, y2[par], ALU.mult, ALU.mult, [R("hs", par), R("rs3", so), R("y2", par)], [R("mxo", k3)])
                DMA("sp", mixa_d[ts_, hc], mxo[k3], [R("mxo", k3)], [R("mixa_d", i, h)], mxo_sem[k3])

            if h + 1 < 4:
                load_qkv(h + 1)
            if h == 0:
                for c in range(2):
                    DMA("pool", w1b_d[c * 512:(c + 1) * 512, :], w1[c * 512:(c + 1) * 512, :], [], [R("w1b", c)], None)
            if h == 1 or h == 2:
                for c in range(2 * (h - 1), 2 * h):
                    DMA("pool", w2b_d[c * 1024:(c + 1) * 1024, :], w2[c * 1024:(c + 1) * 1024, :], [], [R("w2b", c)], None)
            for st_ in range(NT):
                sweep_tile(st_, "F", st_ == 0, st_ - 1, st_ >= NT // 2)
                sweep_tile(NT - 1 - st_, "B", st_ == 0, NT - st_, st_ >= NT // 2)
            SCH.barrier()
            if stop_after <= 2.8:
                return finish(nc, SCH, esem, None)
        ptr[0] = 0
        if stop_after <= 3:
            return finish(nc, SCH, esem, None)

        ptr[0] = base_persist - 8 * S * 2
        w1s = alloc([128, 8, 4 * D], BF16)
        w2s = alloc([128, 32, D], BF16)
        wos = alloc([128, 8, D], BF16)
        wosem = new_dsem()
        wfsem = new_dsem()
        RWO = R("wo")
        RWF = R("wf")
        DMA("pool", wos, w_out.rearrange("(kc p) n -> p kc n", p=128), [], [R("wos")], None)
        RW1 = [R("w1s", 0), R("w1s", 1)]
        RW2 = [R("w2s", 0), R("w2s", 1)]
        for c in range(2):
            DMA("sp", w1s[:, :, c * 2048:(c + 1) * 2048],
                w1b_d[:, c * 2048:(c + 1) * 2048].rearrange("(kc p) n -> p kc n", p=128),
                [R("w1b", 0), R("w1b", 1)], [RW1[c]], new_dsem())
        for c in range(2):
            DMA("sp", w2s[:, c * 16:(c + 1) * 16, :],
                w2b_d[c * 2048:(c + 1) * 2048, :].rearrange("(kc p) n -> p kc n", p=128),
                [R("w2b", 2 * c), R("w2b", 2 * c + 1)], [RW2[c]], new_dsem())
        g1s = alloc([128, D])
        g2s = alloc([128, D])
        nfs = alloc([128, D])
        A2p = alloc([128, 8])
        B2p = alloc([128, 8])
        DMA("sp", g1s, mod_d[2048:3072].partition_broadcast(128), [RMOD], [RWO], wosem)
        DMA("sp", g2s, mod_d[5120:6144].partition_broadcast(128), [RMOD], [RWF], wfsem)
        DMA("sp", nfs, nfgb, [], [RWF], wfsem)
        DMA("sp", A2p, mod_d[4096:5120].rearrange("(c p) -> p c", p=128), [RMOD], [RWF], wfsem,
            allow_slow_non_contiguous=True)
        DMA("sp", B2p, mod_d[3072:4096].rearrange("(c p) -> p c", p=128), [RMOD], [RWF], wfsem,
            allow_slow_non_contiguous=True)
        STT(A2p, A2p, 1.0, n2gs, ALU.add, ALU.mult, [RWF, RC], [R("A2p")])
        base_p4 = ptr[0]
        mxi = [alloc([128, D], BF16) for _ in range(3)]
        mxi_sem = [new_dsem() for _ in range(3)]
        xi = [alloc([128, D]) for _ in range(3)]
        xi_sem = [new_dsem() for _ in range(3)]
        mT = [alloc([128, 8, 128], BF16) for _ in range(2)]
        tpj = [alloc([128, D]) for _ in range(2)]
        x1o = [alloc([128, D]) for _ in range(2)]
        x1o_sem = [new_dsem() for _ in range(2)]
        mai = [alloc([128, D], BF16) for _ in range(3)]
        mai_sem = [new_dsem() for _ in range(3)]

        def RX1(i):
            return R("x1_d", i)

        for i in range(NT):
            k = i % 2
            k3 = i % 3
            ts_ = slice(i * 128, (i + 1) * 128)
            DMA("sp", mxi[k3], mix_d[ts_, :], [RM(i)], [R("mxi", k3)], mxi_sem[k3])
            DMA("sp", xi[k3], x[ts_, :], [], [R("xi", k3)], xi_sem[k3])
            DMA("sp", mai[k3], mixa_d[ts_, :], [R("mixa_d", i, hh_) for hh_ in range(4)], [R("mai", k3)], mai_sem[k3])
            TT("dve", mxi[k3], mxi[k3], mai[k3], ALU.add, [R("mxi", k3), R("mai", k3)], [R("mxi", k3)])
            TRG([(bank_bf(k)[:, kc * 128:(kc + 1) * 128], mxi[k3][:, kc * 128:(kc + 1) * 128]) for kc in range(8)],
                [R("mxi", k3), RC], [PB(k)])
            ACTF(mT[k], bank_bf(k).rearrange("p (a b) -> p a b", a=8, b=128), AF.Copy, [PB(k)], [R("mT", k)])
            for hh in range(2):
                bk = 2 + 2 * k + hh
                cs = slice(hh * 512, (hh + 1) * 512)
                MMG([(banks[bk][:, 0:512], mT[k][:, kc, :], wos[:, kc, cs], kc == 0, kc == 7) for kc in range(8)],
                    [R("mT", k), R("wos")], [PB(bk)])
                TT("dve", tpj[k][:, cs], banks[bk][:, :], g1s[:, cs], ALU.mult, [PB(bk), RWO], [R("tpj", k, hh)])
            TT("pool", x1o[k], tpj[k], xi[k3], ALU.add, [R("tpj", k, 0), R("tpj", k, 1), R("xi", k3)], [R("x1o", k)])
            DMA("sp", x1_d[ts_, :], x1o[k], [R("x1o", k)], [RX1(i)], x1o_sem[k])
        SCH.barrier()
        ptr[0] = base_p4
        if stop_after <= 4:
            return finish(nc, SCH, esem, None)

        xg = [alloc([128, 2, D]) for _ in range(2)]
        xg_sem = [[new_dsem() for _ in range(2)] for _ in range(2)]
        h2t = [alloc([128, D], BF16) for _ in range(2)]
        h2T = [alloc([128, 8, 256], BF16) for _ in range(2)]
        sq4 = [alloc([128, 256]) for _ in range(4)]
        fT = [alloc([128, 256], BF16) for _ in range(4)]
        tf = [alloc([128, D]) for _ in range(2)]
        ob_sem = [new_dsem() for _ in range(2)]
        junk4 = alloc([128, D], BF16)
        sm4 = alloc([128, 256])
        out_ops = []
        NG = S // 256
        for g in range(NG):
            kg = g % 2
            for s_ in range(2):
                i = 2 * g + s_
                ts_ = slice(i * 128, (i + 1) * 128)
                so = (i % 32) * 8
                DMA("sp", xg[kg][:, s_, :], x1_d[ts_, :], [RX1(i)], [R("xg", kg, s_)], xg_sem[kg][s_])
                ACTF(h2t[s_], xg[kg][:, s_, :], AF.Square, [R("xg", kg, s_)], [R("h2t", s_), R("s4", so)],
                     accum_out=sm4[:, so:so + 1])
                TS("dve", sm4[:, so + 1:so + 2], sm4[:, so:so + 1], 1.0 / D, EPS, ALU.mult, ALU.add, [R("s4", so)], [R("v4", so)])
                RSTD(sm4[:, so + 1:so + 2], sm4[:, so + 2:so + 3], R("v4", so), R("r4", so))
                TS("dve", h2t[s_], xg[kg][:, s_, :], sm4[:, so + 2:so + 3], None, ALU.mult, None,
                   [R("xg", kg, s_), R("r4", so)], [R("h2t", s_)])
                TRG([(bank_bf(3)[:, kc * 128:(kc + 1) * 128], h2t[s_][:, kc * 128:(kc + 1) * 128]) for kc in range(8)],
                    [R("h2t", s_), RC], [PB(3)])
                for kc in range(8):
                    ACTF(h2T[kg][:, kc, s_ * 128:(s_ + 1) * 128], bank_bf(3)[:, kc * 128:(kc + 1) * 128], AF.Identity,
                         [PB(3), R("A2p"), RWF], [R("h2T", kg, s_)], scale=A2p[:, kc:kc + 1], bias=B2p[:, kc:kc + 1])
            rh2 = [R("h2T", kg, 0), R("h2T", kg, 1)]

            def ffn1(ft):
                bk = ft % 3
                MMG([(banks[bk][:, 0:256], w1s[:, kc, ft * 128:(ft + 1) * 128], h2T[kg][:, kc, :], kc == 0, kc == 7)
                     for kc in range(8)], rh2 + RW1, [PB(bk)])

            ffn1(0)
            for ft in range(32):
                bk = ft % 4
                pb = ft % 3
                if ft + 1 < 32:
                    ffn1(ft + 1)
                ACTF(sq4[bk], banks[pb][:, 0:256], AF.Square, [PB(pb)], [R("sq4", bk)])
                STT(fT[bk], banks[pb][:, 0:256], 0.0, sq4[bk], ALU.is_gt, ALU.mult, [PB(pb), R("sq4", bk)], [R("fT", bk)])
                for s_ in range(2):
                    for hh in range(2):
                        ob_ = 4 + s_ * 2 + hh
                        MMG([(banks[ob_][:, 0:512], fT[bk][:, s_ * 128:(s_ + 1) * 128], w2s[:, ft, hh * 512:(hh + 1) * 512],
                              ft == 0, ft == 31)], [R("fT", bk)] + RW2, [PB(ob_)])
            for s_ in range(2):
                i = 2 * g + s_
                ts_ = slice(i * 128, (i + 1) * 128)
                so = (i % 32) * 8
                for hh in range(2):
                    ob_ = 4 + s_ * 2 + hh
                    cs = slice(hh * 512, (hh + 1) * 512)
                    TT("dve", tf[s_][:, cs], banks[ob_][:, :], g2s[:, cs], ALU.mult, [PB(ob_), RWF], [R("tf", s_, hh)])
                TT("pool", xg[kg][:, s_, :], tf[s_], xg[kg][:, s_, :], ALU.add,
                   [R("tf", s_, 0), R("tf", s_, 1), R("xg", kg, s_)], [R("xg", kg, s_)])
                ACTF(junk4, xg[kg][:, s_, :], AF.Square, [R("xg", kg, s_)], [R("junk4"), R("s5", so)],
                     accum_out=sm4[:, so + 3:so + 4])
                TS("dve", sm4[:, so + 4:so + 5], sm4[:, so + 3:so + 4], 1.0 / D, EPS, ALU.mult, ALU.add, [R("s5", so)], [R("v5", so)])
                RSTD(sm4[:, so + 4:so + 5], sm4[:, so + 5:so + 6], R("v5", so), R("r5", so))
                STT(tf[s_], xg[kg][:, s_, :], sm4[:, so + 5:so + 6], nfs, ALU.mult, ALU.mult,
                    [R("xg", kg, s_), R("r5", so), RWF], [R("tf", s_, 0), R("tf", s_, 1)])
                out_ops.append(DMA("sp", out[ts_, :], tf[s_], [R("tf", s_, 0), R("tf", s_, 1)], [R("out", i)], ob_sem[s_]))
        return finish(nc, SCH, esem, out_ops)


def finish(nc, SCH, esem, out_ops):
    order = SCH.schedule()

    def replay(e, h):
        for o in order[e]:
            for d in o.waits:
                if d.dsem is not None:
                    h.wait_ge(d.dsem.handle, d.dval)
                else:
                    h.wait_ge(esem[d.eng], d.count)
            if o.fn is None:
                if o.signal:
                    h.nop().then_inc(esem[e], 1)
                continue
            ins = o.fn(h)
            if o.dsem is not None:
                ins.then_inc(o.dsem.handle, 16)
            elif o.signal:
                ins.then_inc(esem[e], 1)

    with nc.Block() as block:
        @block.tensor
        def _(h):
            replay("pe", h)

        @block.scalar
        def _(h):
            replay("act", h)

        @block.vector
        def _(h):
            replay("dve", h)

        @block.gpsimd
        def _(h):
            replay("pool", h)

        @block.sync
        def _(h):
            replay("sp", h)
    return nc


def host_inputs(b, x, c, w_ada, b_ada, norm1_g, norm2_g, w_in, b_if, conv_w, conv_b, mh_g,
                ln_v_g, ln_v_b, w_s, b_s, w_out, w1, w2, normf_g):
    f = np.float32

    def bc(v):
        return np.ascontiguousarray(np.broadcast_to(np.asarray(v, f).reshape(1, -1), (128, v.size)))

    r = np.arange(128)
    mF = (r[:, None] <= r[None, :]).astype(f)
    mB = (r[:, None] >= r[None, :]).astype(f)
    cm = np.concatenate([mF, mB, np.ones((128, 128), f), mF / 16.0, mB / 16.0], axis=1)
    cw = np.asarray(conv_w[0], f)
    cwp = np.ascontiguousarray(cw.reshape(3, 16, 128).transpose(2, 1, 0).reshape(128, 48))
    cbp = np.ascontiguousarray(np.asarray(conv_b[0], f).reshape(16, 128).T)
    return {
        "x": np.ascontiguousarray(x[b], dtype=f),
        "c_b": np.ascontiguousarray(np.asarray(c[b], f).reshape(8, 128).T),
        "w_ada": np.ascontiguousarray(w_ada[0], dtype=f),
        "b_ada": np.ascontiguousarray(b_ada[0:1], dtype=f),
        "n1gb": bc(norm1_g[0]),
        "n2gp": np.ascontiguousarray(np.asarray(norm2_g[0], f).reshape(8, 128).T),
        "w_in": np.ascontiguousarray(w_in[0], dtype=f),
        "bifb": np.ascontiguousarray(np.broadcast_to(np.asarray(b_if[0], f).reshape(1, 1, 16), (128, 32, 16)).reshape(128, 512)),
        "cwp": cwp,
        "cbp": cbp,
        "mhgb": bc(mh_g[0]),
        "lngb": bc(ln_v_g[0]),
        "lnbb": bc(ln_v_b[0]),
        "wsT": np.ascontiguousarray(np.asarray(w_s[0], f).transpose(2, 0, 1).reshape(128, 1024)),
        "bsp": np.ascontiguousarray(np.asarray(b_s[0], f).T),
        "w_out": np.ascontiguousarray(w_out[0], dtype=f),
        "w1": np.ascontiguousarray(w1[0], dtype=f),
        "w2": np.ascontiguousarray(w2[0], dtype=f),
        "nfgb": bc(normf_g),
        "cmask": cm,
        "identb": np.eye(128, dtype=f).astype(ml_dtypes.bfloat16),
    }


def kernel(**inputs):
    inputs = {k: np.asarray(v) for k, v in inputs.items()}
    nc = build_program()
    in_maps = [host_inputs(b, **inputs) for b in range(8)]
    res = run_bass_kernel_spmd(nc, in_maps, core_ids=list(range(8)))
    return np.stack([np.asarray(r["out"], dtype=np.float32) for r in res.results], axis=0)
```

```python
import numpy as np
import ml_dtypes
from contextlib import ExitStack
import concourse.bass as bass
import concourse.mybir as mybir
from concourse.bass_utils import run_bass_kernel_spmd

F32 = mybir.dt.float32
BF16 = mybir.dt.bfloat16
ALU = mybir.AluOpType
AF = mybir.ActivationFunctionType

D = 1024
S = 4096
NT = 32
DIN = 8208
EPS = 1e-6
ARENA_WORDS = 52992
ENGS = ["pe", "act", "dve", "pool", "sp"]


class Res:
    __slots__ = ("name", "lw", "rd")

    def __init__(self, name):
        self.name = name
        self.lw = None
        self.rd = []


class DSem:
    __slots__ = ("handle", "n", "last")

    def __init__(self, handle):
        self.handle = handle
        self.n = 0
        self.last = None


class Op:
    __slots__ = ("eng", "fn", "deps", "gid", "seg", "signal", "dsem", "dval", "count", "occ", "lat",
                 "pos", "fin", "nrem", "succ", "rdy", "waits", "is_bar", "line", "soft", "_excl")


SYNC_LAT = 0.35


class Sched:
    def __init__(self):
        self.all = []
        self.res = {}
        self.seg = 0

    def R(self, *key):
        r = self.res.get(key)
        if r is None:
            r = Res(key)
            self.res[key] = r
        return r

    def op(self, eng, fn, reads=(), writes=(), dsem=None, occ=0.3, lat=None):
        o = Op()
        o.eng = eng
        o.fn = fn
        o.gid = len(self.all)
        o.seg = self.seg
        o.signal = False
        o.dsem = dsem
        o.dval = 0
        o.count = 0
        o.occ = occ
        o.lat = occ if lat is None else lat
        o.is_bar = False
        import sys as _sys
        f_ = _sys._getframe(1)
        while f_.f_code.co_name != "build_program" and f_.f_back is not None and f_.f_code.co_name not in ("sweep_tile", "p2_proj", "ffn1"):
            f_ = f_.f_back
        o.line = f_.f_lineno
        deps = {}
        soft = set()
        hard = set()

        def add(d, is_soft=False):
            if d is not None and d.seg == o.seg:
                deps[d.gid] = d
                (soft if is_soft else hard).add(d.gid)

        excl = [r for r in reads if r.name[0] == "bank" and r not in writes]
        if excl:
            reads = [r for r in reads if r.name[0] != "bank"]
        for r in reads:
            add(r.lw)
        for w in writes:
            add(w.lw)
            for rr in w.rd:
                add(rr)
        for w in excl:
            lw = w.lw
            if lw is not None:
                add(lw, is_soft=(lw.eng == eng and w in getattr(lw, "_excl", ())))
            for rr in w.rd:
                add(rr)
        o._excl = excl
        writes = list(writes) + excl
        if dsem is not None:
            add(dsem.last, is_soft=True)
            dsem.n += 1
            o.dval = 16 * dsem.n
            dsem.last = o
        o.soft = soft - hard
        o.deps = list(deps.values())
        for r in reads:
            if r.rd and r.rd[-1].seg != o.seg:
                r.rd = []
            r.rd.append(o)
        for w in writes:
            w.lw = o
            w.rd = []
        self.all.append(o)
        return o

    def barrier(self):
        self.seg += 1

    def schedule(self):
        import heapq
        nseg = self.seg + 1
        segs = [[] for _ in range(nseg)]
        for o in self.all:
            segs[o.seg].append(o)
        order = {e: [] for e in ENGS}
        for si, ops in enumerate(segs):
            for o in ops:
                o.succ = []
                o.nrem = len(o.deps)
                o.rdy = 0.0
            for o in ops:
                for d in o.deps:
                    d.succ.append(o)
            ready = {e: [] for e in ENGS}
            for o in ops:
                if o.nrem == 0:
                    heapq.heappush(ready[o.eng], (0.0, o.gid, o))
            free = {e: 0.0 for e in ENGS}
            seg_order = {e: [] for e in ENGS}
            left = len(ops)
            while left:
                best = None
                for e in ENGS:
                    if ready[e]:
                        rt, gid, o = ready[e][0]
                        st = max(rt, free[e])
                        if best is None or (st, gid) < (best[0], best[1]):
                            best = (st, gid, o)
                st, gid, o = best
                heapq.heappop(ready[o.eng])
                free[o.eng] = st + o.occ
                o.fin = st + o.lat
                seg_order[o.eng].append(o)
                left -= 1
                for sc in o.succ:
                    lat = 0.0 if (sc.eng == o.eng and o.dsem is None) else SYNC_LAT
                    sc.rdy = max(sc.rdy, o.fin + lat)
                    sc.nrem -= 1
                    if sc.nrem == 0:
                        heapq.heappush(ready[sc.eng], (sc.rdy, sc.gid, sc))
            last_dma = {}
            for o in ops:
                if o.dsem is not None:
                    last_dma[o.dsem] = o
            bar = Op()
            bar.eng = "sp"; bar.fn = None; bar.gid = -1; bar.seg = si; bar.signal = False
            bar.dsem = None; bar.dval = 0; bar.count = 0; bar.is_bar = True
            bar.deps = [seg_order[e][-1] for e in ENGS if e != "sp" and seg_order[e]] + list(last_dma.values())
            seg_order["sp"].append(bar)
            if si + 1 < nseg:
                for e in ENGS:
                    if e != "sp":
                        b2 = Op()
                        b2.eng = e; b2.fn = None; b2.gid = -1; b2.seg = si; b2.signal = False
                        b2.dsem = None; b2.dval = 0; b2.count = 0; b2.is_bar = True
                        b2.deps = [bar]
                        seg_order[e].append(b2)
            for e in ENGS:
                order[e] += seg_order[e]
        for e in ENGS:
            for p, o in enumerate(order[e]):
                o.pos = p
        for e in ENGS:
            seen = {}
            for o in order[e]:
                need = {}
                for d in o.deps:
                    if d.gid in getattr(o, "soft", ()):
                        continue
                    if d.dsem is not None:
                        key, val = d.dsem, d.dval
                    else:
                        if d.eng == e and (e == "pe" or e == "sp" or d.gid in getattr(o, "soft", ())):
                            continue
                        key, val = d.eng, d.pos
                    if seen.get(key, -1) >= val:
                        continue
                    if key not in need or need[key][0] < val:
                        need[key] = (val, d)
                o.waits = []
                for key, (val, d) in need.items():
                    seen[key] = val
                    if d.dsem is None:
                        d.signal = True
                    o.waits.append(d)
        for e in ENGS:
            c = 0
            for o in order[e]:
                if o.signal and o.dsem is None:
                    c += 1
                    o.count = c
        return order


def build_program(debug=False, stop_after=99):
    nc = bass.Bass("TRN2", target_bir_lowering=False)
    SCH = Sched()
    R = SCH.R

    def din(name, shape, dt=F32):
        return nc.dram_tensor(name, list(shape), dt, kind="ExternalInput").ap()

    x = din("x", [S, D])
    c_b = din("c_b", [128, 8])
    w_ada = din("w_ada", [D, 6 * D])
    b_ada = din("b_ada", [1, 6 * D])
    n1gb = din("n1gb", [128, D])
    n2gp = din("n2gp", [128, 8])
    w_in = din("w_in", [D, DIN])
    bifb = din("bifb", [128, 512])
    cwp = din("cwp", [128, 48])
    cbp = din("cbp", [128, 16])
    mhgb = din("mhgb", [128, D])
    lngb = din("lngb", [128, D])
    lnbb = din("lnbb", [128, D])
    wsT = din("wsT", [128, 1024])
    bsp = din("bsp", [128, 8])
    w_out = din("w_out", [D, D])
    w1 = din("w1", [D, 4 * D])
    w2 = din("w2", [4 * D, D])
    nfgb = din("nfgb", [128, D])
    cmask = din("cmask", [128, 640])
    identb = din("identb", [128, 128], BF16)
    out = nc.dram_tensor("out", [S, D], F32, kind="ExternalOutput").ap()
    sk = "ExternalOutput" if debug else "Internal"
    mod_d = nc.dram_tensor("mod_d", [6 * D], F32, kind=sk).ap()
    mix_d = nc.dram_tensor("mix_d", [S, D], BF16, kind=sk).ap()
    hf_d = nc.dram_tensor("hf_d", [S, 256], F32, kind=sk).ap()
    mixa_d = nc.dram_tensor("mixa_d", [S, D], BF16, kind=sk).ap()
    x1_d = nc.dram_tensor("x1_d", [S, D], F32, kind=sk).ap()

    es = ExitStack()
    with es:
        arena = es.enter_context(nc.sbuf_tensor("arena", [128, ARENA_WORDS], F32))
        banks = [es.enter_context(nc.psum_tensor(f"ps{i}", [128, 512], F32)) for i in range(8)]
        esem = {e: es.enter_context(nc.semaphore(f"sem_{e}")) for e in ENGS}
        nsem = [0]

        def new_dsem():
            nsem[0] += 1
            return DSem(es.enter_context(nc.semaphore(f"dsem{nsem[0]}")))

        ptr = [0]

        def alloc(shape, dt=F32):
            n = 1
            for s_ in shape[1:]:
                n *= s_
            nb = n * (4 if dt == F32 else 2)
            nb = (nb + 31) // 32 * 32
            off = ptr[0]
            ptr[0] += nb
            assert ptr[0] <= max(top[0], off + nb) and ptr[0] <= ARENA_WORDS * 4, ("arena overflow", ptr[0], top[0])
            ap = arena[:, off // 4:(off + nb) // 4]
            if dt != F32:
                ap = ap.bitcast(dt)
            ap = ap[:, 0:n]
            if len(shape) == 3:
                ap = ap.rearrange("p (a b) -> p a b", a=shape[1], b=shape[2])
            return ap

        top = [ARENA_WORDS * 4]

        def alloc_top(shape, dt=F32):
            n = 1
            for s_ in shape[1:]:
                n *= s_
            nb = (n * (4 if dt == F32 else 2) + 31) // 32 * 32
            top[0] -= nb
            save = ptr[0]
            ptr[0] = top[0]
            ap = alloc(shape, dt)
            ptr[0] = save
            return ap

        def bank_bf(i):
            return banks[i][:, :].bitcast(BF16)

        def PB(i):
            return R("bank", i)

        def nfree(ap):
            n = 1
            for s_ in ap.shape[1:]:
                n *= s_
            return n

        def E(eng, fn, reads=(), writes=(), occ=0.3):
            return SCH.op(eng, fn, reads, writes, occ=occ)

        def DMA(q, o_ap, i_ap, reads, writes, sem, **kw):
            if q == "pool":
                sem = new_dsem()
            nbytes = nfree(o_ap) * o_ap.shape[0] * (4 if o_ap.dtype == F32 else 2)
            return SCH.op(q, lambda e, o_ap=o_ap, i_ap=i_ap, kw=kw: e.dma_start(out=o_ap, in_=i_ap, **kw),
                          reads, writes, dsem=sem, occ=0.4 if q == "sp" else 1.5, lat=3.0 + nbytes / 120e3)

        def MMG(items, reads, writes):
            def fn(e, items=items):
                ins = None
                for (o_, l_, r_, st, sp) in items:
                    ins = e.matmul(o_, l_, r_, start=st, stop=sp)
                return ins
            t = 0.0
            for (o_, l_, r_, st, sp) in items:
                t += max(64, nfree(r_)) * 0.00043 * (4 if l_.dtype == F32 else 1) + 0.01
            return SCH.op("pe", fn, reads, writes, occ=t, lat=t + 0.15)

        def TRG(items, reads, writes):
            def fn(e, items=items):
                ins = None
                for (o_, i_) in items:
                    ins = e.transpose(o_, i_, ident)
                return ins
            t = 0.11 * len(items)
            return SCH.op("pe", fn, reads, writes, occ=t, lat=t + 0.15)

        def ACTF(o_, i_, func, reads, writes, eng="act", **kw):
            t = 0.22 + nfree(o_) * 0.00085 + (0.1 if "accum_out" in kw else 0.0)
            return SCH.op("act", lambda e, o_=o_, i_=i_, func=func, kw=kw: e.activation(o_, i_, func, **kw),
                          reads, writes, occ=t)

        def vcost(eng, o_):
            return (0.12 + nfree(o_) * 0.00104) if eng == "dve" else (0.15 + nfree(o_) * 0.0022)

        def TS(eng, o_, i_, s1, s2, op0, op1, reads, writes):
            if op1 is None:
                return E(eng, lambda e, o_=o_, i_=i_: e.tensor_scalar(o_, i_, s1, None, op0), reads, writes, vcost(eng, o_))
            return E(eng, lambda e, o_=o_, i_=i_: e.tensor_scalar(o_, i_, s1, s2, op0, op1), reads, writes, vcost(eng, o_))

        def TT(eng, o_, a_, b_, op, reads, writes):
            return E(eng, lambda e, o_=o_, a_=a_, b_=b_, op=op: e.tensor_tensor(o_, a_, b_, op), reads, writes,
                     vcost(eng, o_))

        def STT(o_, a_, sc, b_, op0, op1, reads, writes):
            return E("dve", lambda e, o_=o_, a_=a_, sc=sc, b_=b_, op0=op0, op1=op1:
                     e.scalar_tensor_tensor(o_, a_, sc, b_, op0, op1), reads, writes, vcost("dve", o_))

        def RSTD(ve_ap, out_ap, r_in, r_out):
            return TT("pool", out_ap, ve_ap, neghalf[:, 0:1], ALU.pow, [r_in, RC], [r_out])

        csem = new_dsem()
        RC = R("consts")
        ident = alloc([128, 128], BF16)
        cm = alloc([128, 640])
        maskF = cm[:, 0:128]
        maskB = cm[:, 128:256]
        ones = cm[:, 256:384]
        maskFq = cm[:, 384:512]
        maskBq = cm[:, 512:640]
        neghalf = alloc([128, 8])
        cB = alloc([128, 8])
        bifs = alloc([128, 512])
        cws = alloc([128, 48])
        cbs = alloc([128, 16])
        bss = alloc([128, 8])
        n2gs = alloc([128, 8])
        for (dst, src) in [(ident, identb), (cm, cmask), (cB, c_b), (bifs, bifb), (cws, cwp), (cbs, cbp),
                           (bss, bsp), (n2gs, n2gp)]:
            DMA("sp", dst, src, [], [RC], csem)
        E("pool", lambda e: e.memset(neghalf, -0.5), [], [RC])
        hT = alloc([128, 8, S], BF16)
        base_persist = ptr[0]

        def RH(i):
            return R("hT", i)

        cact = alloc([128, 8])
        csig = alloc([128, 8])
        bada_s = alloc([1, 6 * D])
        NW0 = 4
        rowb = [alloc([1, 512]) for _ in range(NW0)]
        wst = [alloc([128, 8, 512]) for _ in range(NW0)]
        wst_sem = [new_dsem() for _ in range(NW0)]
        row_sem = [new_dsem() for _ in range(NW0)]
        DMA("sp", bada_s[0:1, :], b_ada[0:1, :], [], [RC], csem)
        ACTF(cact, cB, AF.Silu, [RC], [R("cact")])
        RMOD = R("mod_d")
        for nb in range(12):
            b = nb % NW0
            DMA("sp", wst[b], w_ada[:, nb * 512:(nb + 1) * 512].rearrange("(kc p) n -> p kc n", p=128),
                [], [R("wst", b)], wst_sem[b])
            MMG([(banks[b][0:1, 0:512], cact[:, kc:kc + 1], wst[b][:, kc, :], kc == 0, kc == 7) for kc in range(8)],
                [R("cact"), R("wst", b)], [PB(b)])
            TT("dve", rowb[b][0:1, :], banks[b][0:1, 0:512], bada_s[0:1, nb * 512:(nb + 1) * 512], ALU.add,
               [PB(b), RC], [R("rowb", b)])
            DMA("sp", mod_d[nb * 512:(nb + 1) * 512].rearrange("(o n) -> o n", o=1), rowb[b][0:1, :],
                [R("rowb", b)], [RMOD], row_sem[b])
        SCH.barrier()
        ptr[0] = base_persist

        wsgu = alloc([128, 8, 3072], BF16)
        wsgu_sem = new_dsem()
        RW = R("lnc")
        RWs = [R("wsgu", 0), R("wsgu", 1)]
        wss = alloc([128, 8, 128], BF16)

        def p2_weight_loads(after):
            DMA("pool", wsgu[:, :, 0:2048], w_in[:, 4096:6144].rearrange("(kc p) n -> p kc n", p=128), after, [RWs[0]], None)
            DMA("pool", wsgu[:, :, 2048:3072], w_in[:, 7168:8192].rearrange("(kc p) n -> p kc n", p=128), after, [RWs[1]], None)
            DMA("pool", wss, wsT.rearrange("p (g q) -> p g q", g=8), after, [R("wss")], None)
        lgs = alloc([128, D])
        lbs = alloc([128, D])
        DMA("sp", lgs, lngb, [], [RW], wsgu_sem)
        DMA("sp", lbs, lnbb, [], [RW], wsgu_sem)
        base_p2w = ptr[0]
        A1b = alloc([128, D])
        B1b = alloc([128, D])
        n1s = alloc([128, D])
        p1sem = new_dsem()
        DMA("sp", n1s, n1gb, [], [R("p1c")], p1sem)
        DMA("sp", A1b, mod_d[1024:2048].partition_broadcast(128), [RMOD], [R("p1c")], p1sem)
        DMA("sp", B1b, mod_d[0:1024].partition_broadcast(128), [RMOD], [R("p1c")], p1sem)
        STT(A1b, A1b, 1.0, n1s, ALU.add, ALU.mult, [R("p1c")], [R("A1b")])
        NXB = 6
        xb = [alloc([128, D]) for _ in range(NXB)]
        xsem = [new_dsem() for _ in range(NXB)]
        junk = alloc([128, D], BF16)
        ss = alloc([128, 64])
        t1 = [alloc([128, D]) for _ in range(2)]
        xh = [alloc([128, D], BF16) for _ in range(2)]
        for i in range(NT):
            k3, k2 = i % NXB, i % 2
            DMA("sp", xb[k3], x[i * 128:(i + 1) * 128, :], [], [R("xb", k3)], xsem[k3])
            ACTF(junk, xb[k3], AF.Square, [R("xb", k3)], [R("junk"), R("ss", i)], accum_out=ss[:, i:i + 1])
            if i == 5:
                p2_weight_loads([R("ss", 5)])
            TS("dve", ss[:, 32 + i:33 + i], ss[:, i:i + 1], 1.0 / D, EPS, ALU.mult, ALU.add, [R("ss", i)], [R("ve", i)])
            RSTD(ss[:, 32 + i:33 + i], ss[:, i:i + 1], R("ve", i), R("rs", i))
            STT(t1[k2], xb[k3], ss[:, i:i + 1], A1b, ALU.mult, ALU.mult, [R("xb", k3), R("rs", i), R("A1b")], [R("t1", k2)])
            TT("dve", xh[k2], t1[k2], B1b, ALU.add, [R("t1", k2), R("p1c")], [R("xh", k2)])
            TRG([(bank_bf(k2)[:, kc * 128:(kc + 1) * 128], xh[k2][:, kc * 128:(kc + 1) * 128]) for kc in range(8)],
                [R("xh", k2), RC], [PB(k2)])
            ACTF(hT[:, :, i * 128:(i + 1) * 128], bank_bf(k2).rearrange("p (a b) -> p a b", a=8, b=128), AF.Copy,
                 [PB(k2)], [RH(i)])
        SCH.barrier()
        ptr[0] = base_persist
        if stop_after <= 1:
            return finish(nc, SCH, esem, None)

        ptr[0] = base_p2w
        wg = alloc_top([128, 8, 16], BF16)
        mhs = alloc_top([128, D])
        wqk_top = alloc_top([128, 8, 512], BF16)
        wv_top = alloc_top([128, 8, 256], BF16)
        p3sem = new_dsem()
        RP3 = R("p3c")
        DMA("pool", wg, w_in[:, 8192:8208].rearrange("(kc p) n -> p kc n", p=128), [], [R("wg")], None)
        DMA("sp", mhs, mhgb, [], [RP3], p3sem)
        for wi, (dst, c0) in enumerate([(wqk_top[:, :, 0:256], 0), (wqk_top[:, :, 256:512], 1024), (wv_top, 2048)]):
            DMA("pool", dst, w_in[:, c0:c0 + 256].rearrange("(kc p) n -> p kc n", p=128), [], [R("wh", 0, wi)], None)
        sg = [alloc([128, D]) for _ in range(2)]
        gu = [alloc([128, D]) for _ in range(2)]
        gv = [alloc([128, D]) for _ in range(2)]
        tln = alloc([128, D])
        vn = [alloc([128, D], BF16) for _ in range(2)]
        yb = alloc([128, D])
        mxb = [alloc([128, D], BF16) for _ in range(2)]
        mxb_sem = [new_dsem() for _ in range(2)]
        st2 = alloc([128, 32 * 16])

        def RM(i):
            return R("mix_d", i)

        def p2_proj(i):
            for g in range(6):
                MMG([(banks[g][:, 0:512], hT[:, kc, i * 128:(i + 1) * 128], wsgu[:, kc, g * 512:(g + 1) * 512],
                      kc == 0, kc == 7) for kc in range(8)], [RH(i)] + RWs, [PB(g)])

        p2_proj(0)
        for i in range(NT):
            k = i % 2
            sb = i * 16
            def act_u():
                for hh in range(2):
                    cs = slice(hh * 512, (hh + 1) * 512)
                    ACTF(gu[k][:, cs], banks[hh][:, :], AF.Gelu, [PB(hh)], [R("gu", k, hh)])

            def act_v():
                for hh in range(2):
                    cs = slice(hh * 512, (hh + 1) * 512)
                    ACTF(gv[k][:, cs], banks[2 + hh][:, :], AF.Gelu, [PB(2 + hh)], [R("gv", k, hh)])
                    E("dve", lambda e, o_=st2[:, sb + hh * 6:sb + hh * 6 + 6], i_=gv[k][:, cs]: e.bn_stats(o_, i_),
                      [R("gv", k, hh)], [R("st2", i, hh)])

            def act_m():
                for hh in range(2):
                    cs = slice(hh * 512, (hh + 1) * 512)
                    ACTF(sg[k][:, cs], banks[4 + hh][:, :], AF.Sigmoid, [PB(4 + hh)], [R("sg", k, hh)])

            if i % 2 == 0:
                act_u(); act_v(); act_m()
            else:
                act_m(); act_v(); act_u()
            if i + 1 < NT:
                p2_proj(i + 1)
            E("dve", lambda e, o_=st2[:, sb + 12:sb + 14], i_=st2[:, sb:sb + 12]: e.bn_aggr(o_, i_),
              [R("st2", i, 0), R("st2", i, 1)], [R("mv", i)])
            TS("dve", st2[:, sb + 14:sb + 15], st2[:, sb + 13:sb + 14], EPS, None, ALU.add, None,
               [R("mv", i)], [R("ve2", i)])
            RSTD(st2[:, sb + 14:sb + 15], st2[:, sb + 15:sb + 16], R("ve2", i), R("rs2", i))
            STT(tln, gv[k], st2[:, sb + 12:sb + 13], lgs, ALU.subtract, ALU.mult,
                [R("gv", k, 0), R("gv", k, 1), R("mv", i), RW], [R("tln")])
            STT(vn[k], tln, st2[:, sb + 15:sb + 16], lbs, ALU.mult, ALU.add, [R("tln"), R("rs2", i), RW], [R("vn", k)])
            for g8 in range(8):
                bk = 6 + g8 // 4
                MMG([(banks[bk][:, (g8 % 4) * 128:(g8 % 4 + 1) * 128], wss[:, g8, :], vn[k][:, g8 * 128:(g8 + 1) * 128],
                      True, True)], [R("vn", k), R("wss")], [PB(bk)])
            for g8 in range(8):
                bk = 6 + g8 // 4
                gs = slice(g8 * 128, (g8 + 1) * 128)
                STT(yb[:, gs], banks[bk][:, (g8 % 4) * 128:(g8 % 4 + 1) * 128], bss[:, g8:g8 + 1], gu[k][:, gs],
                    ALU.add, ALU.mult, [PB(bk), RC, R("gu", k, g8 // 4)], [R("yb", g8)])
            TT("pool", mxb[k], yb, sg[k], ALU.mult,
               [R("yb", g8) for g8 in range(8)] + [R("sg", k, 0), R("sg", k, 1)], [R("mxb", k)])
            DMA("sp", mix_d[i * 128:(i + 1) * 128, :], mxb[k], [R("mxb", k)], [RM(i)], mxb_sem[k])
        SCH.barrier()
        ptr[0] = base_persist
        if stop_after <= 2:
            return finish(nc, SCH, esem, None)

        G = alloc([128, 32, 16])
        SPf = alloc([128, 128])
        SPb = alloc([128, 128])
        tmpe = alloc([128, 128])
        gsc = {nm: alloc([128, 128]) for nm in ["wF", "uF", "sF", "wB", "uB", "sB", "sF16", "sB16"]}
        RG = R("gates")
        for i in range(NT):
            MMG([(banks[0][:, i * 16:(i + 1) * 16], hT[:, kc, i * 128:(i + 1) * 128], wg[:, kc, :], kc == 0, kc == 7)
                 for kc in range(8)], [RH(i), R("wg")], [PB(0)])
        TT("dve", G.rearrange("p a b -> p (a b)"), banks[0][:, :], bifs, ALU.add, [PB(0), RC], [RG])
        for (SPx, c0, key) in [(SPf, 4, "SPf"), (SPb, 12, "SPb")]:
            ACTF(tmpe.rearrange("p (a b) -> p a b", a=32, b=4), G[:, :, c0:c0 + 4], AF.Exp, [RG], [R("tmpe")], scale=-1.0)
            ACTF(SPx, tmpe, AF.Ln, [R("tmpe")], [R(key)], bias=1.0)
        MMG([(banks[1][:, 0:128], maskF, SPf, True, True)], [R("SPf"), RC], [PB(1)])
        MMG([(banks[1][:, 128:256], maskB, SPb, True, True)], [R("SPb"), RC], [PB(1)])
        MMG([(banks[1][:, 256:384], ones, SPf, True, True)], [R("SPf"), RC], [PB(1)])
        MMG([(banks[1][:, 384:512], ones, SPb, True, True)], [R("SPb"), RC], [PB(1)])
        RGS = R("gsc")
        ACTF(gsc["wF"], banks[1][:, 0:128], AF.Exp, [PB(1)], [RGS], scale=-1.0)
        ACTF(gsc["wB"], banks[1][:, 128:256], AF.Exp, [PB(1)], [RGS], scale=-1.0)
        ACTF(gsc["sF"], banks[1][:, 256:384], AF.Exp, [PB(1)], [RGS], scale=-1.0)
        ACTF(gsc["sB"], banks[1][:, 384:512], AF.Exp, [PB(1)], [RGS], scale=-1.0)
        TS("dve", gsc["sF16"], gsc["sF"], 1.0 / 16.0, None, ALU.mult, None, [RGS], [R("gs16", 0)])
        TS("dve", gsc["sB16"], gsc["sB"], 1.0 / 16.0, None, ALU.mult, None, [RGS], [R("gs16", 1)])
        TT("dve", tmpe.rearrange("p (a b) -> p a b", a=32, b=4), banks[1][:, 0:128].rearrange("p (a b) -> p a b", a=32, b=4),
           G[:, :, 0:4], ALU.add, [PB(1), RG, R("SPf"), R("SPb")], [R("tmpe")])
        ACTF(gsc["uF"], tmpe, AF.Exp, [R("tmpe")], [RGS])
        TT("dve", tmpe.rearrange("p (a b) -> p a b", a=32, b=4), banks[1][:, 128:256].rearrange("p (a b) -> p a b", a=32, b=4),
           G[:, :, 8:12], ALU.add, [PB(1), RG, RGS], [R("tmpe")])
        ACTF(gsc["uB"], tmpe, AF.Exp, [R("tmpe")], [RGS])

        SCH.barrier()
        if stop_after <= 2.2:
            return finish(nc, SCH, esem, None)
        qT = alloc([128, 2, S], BF16)
        kT = alloc([128, 2, S], BF16)
        vext = alloc([128, 32, 258], BF16)
        E("pool", lambda e: e.memset(vext[:, :, 256:258], 1.0), [], [R("vones")])
        wqk2 = [wqk_top, alloc([128, 8, 512], BF16)]
        wv2 = [wv_top, alloc([128, 8, 256], BF16)]
        wog = alloc([128, 8, 512], BF16)
        Chat = {d_: alloc([128, 2, 257]) for d_ in "FB"}
        Cbf = {d_: alloc([128, 2, 258], BF16) for d_ in "FB"}
        base_p3 = ptr[0]
        zst = alloc([128, 4098])
        acc = alloc([128, 4096])
        end_conv = ptr[0]
        ptr[0] = base_p3
        PT = [alloc([128, 128], BF16) for _ in range(2)]
        kw = [alloc([128, 256], BF16) for _ in range(2)]
        nds = [alloc([128, 257]) for _ in range(2)]
        sml = alloc([128, 64 * 4])
        hfo = [alloc([128, 256]) for _ in range(4)]
        hfo_sem = [new_dsem() for _ in range(4)]
        hfi = [alloc([128, 256]) for _ in range(4)]
        hfi_sem = [new_dsem() for _ in range(4)]
        mbi = [alloc([128, 256], BF16) for _ in range(4)]
        mbi_sem = [new_dsem() for _ in range(4)]
        mxo = [alloc([128, 256], BF16) for _ in range(4)]
        mxo_sem = [new_dsem() for _ in range(4)]
        sgo = [alloc([128, 512]) for _ in range(2)]
        hs = [alloc([128, 256]) for _ in range(2)]
        y1 = [alloc([128, 256]) for _ in range(2)]
        y2 = [alloc([128, 256]) for _ in range(2)]
        junk3 = alloc([128, 256], BF16)
        ptr[0] = max(ptr[0], end_conv)

        def RHF(i):
            return R("hf_d", i)

        for h in range(4):
            hc = slice(h * 256, (h + 1) * 256)
            wqk = wqk2[h % 2]
            wv = wv2[h % 2]

            def load_qkv(hh_):
                for wi, (dst, c0) in enumerate([(wqk2[hh_ % 2][:, :, 0:256], hh_ * 256),
                                                (wqk2[hh_ % 2][:, :, 256:512], 1024 + hh_ * 256),
                                                (wv2[hh_ % 2], 2048 + hh_ * 256)]):
                    DMA("pool", dst, w_in[:, c0:c0 + 256].rearrange("(kc p) n -> p kc n", p=128), [],
                        [R("wh", hh_ % 2, wi)], None)

            for wi, (dst, c0) in enumerate([(wog[:, :, 0:256], 3072 + h * 256), (wog[:, :, 256:512], 6144 + h * 256)]):
                DMA("pool", dst, w_in[:, c0:c0 + 256].rearrange("(kc p) n -> p kc n", p=128), [], [R("wog", wi)], None)
            RWQK = [R("wh", h % 2, 0), R("wh", h % 2, 1)]
            RWV = [R("wh", h % 2, 2)]
            RWOG = [R("wog", 0), R("wog", 1)]
            E("pool", lambda e: e.memset(zst[:, 0:1], 0.0), [], [R("zpad")])
            E("pool", lambda e: e.memset(zst[:, 4097:4098], 0.0), [], [R("zpad")])
            for ct in range(4):
                cti = (0 if ct < 2 else 8) + h * 2 + (ct % 2)
                for tb in range(8):
                    bk = tb % 2
                    MMG([(banks[bk][:, 0:512], wqk[:, kc, ct * 128:(ct + 1) * 128], hT[:, kc, tb * 512:(tb + 1) * 512],
                          kc == 0, kc == 7) for kc in range(8)], [RH(4 * tb + j) for j in range(4)] + RWQK, [PB(bk)])
                    if tb % 2 == 0:
                        ACTF(zst[:, 1 + tb * 512:1 + (tb + 1) * 512], banks[bk][:, :], AF.Copy, [PB(bk)], [R("z", tb)])
                    else:
                        E("dve", lambda e, o_=zst[:, 1 + tb * 512:1 + (tb + 1) * 512], i_=banks[bk][:, :]: e.tensor_copy(o_, i_),
                          [PB(bk)], [R("z", tb)], 0.65)
                for hf2 in range(2):
                    a0 = hf2 * 2048
                    zr = [R("z", tb) for tb in range(8)] + [R("zpad")]
                    ACTF(acc[:, a0:a0 + 2048], zst[:, 1 + a0:1 + a0 + 2048], AF.Identity, zr + [RC], [R("acc", hf2)],
                         scale=cws[:, cti * 3 + 1:cti * 3 + 2], bias=cbs[:, cti:cti + 1])
                    STT(acc[:, a0:a0 + 2048], zst[:, a0:a0 + 2048], cws[:, cti * 3:cti * 3 + 1], acc[:, a0:a0 + 2048],
                        ALU.mult, ALU.add, zr + [RC, R("acc", hf2)], [R("acc", hf2)])
                    STT(acc[:, a0:a0 + 2048], zst[:, 2 + a0:2 + a0 + 2048], cws[:, cti * 3 + 2:cti * 3 + 3], acc[:, a0:a0 + 2048],
                        ALU.mult, ALU.add, zr + [RC, R("acc", hf2)], [R("acc", hf2)])
                    dst = (qT if ct < 2 else kT)[:, ct % 2, a0:a0 + 2048]
                    rdst = [R("qk", ct, 16 * hf2 + j) for j in range(16)]
                    ACTF(dst, acc[:, a0:a0 + 2048], AF.Silu, [R("acc", hf2)], rdst)
            for i in range(NT):
                bk = 2 + i % 2
                MMG([(banks[bk][:, 0:256], hT[:, kc, i * 128:(i + 1) * 128], wv[:, kc, :], kc == 0, kc == 7)
                     for kc in range(8)], [RH(i)] + RWV, [PB(bk)])
                ACTF(vext[:, i, 0:256], banks[bk][:, 0:256], AF.Copy, [PB(bk)], [R("v", i)])
            SCH.barrier()
            if stop_after <= 2.4:
                return finish(nc, SCH, esem, None)

            def sweep_tile(i, dirn, first, prev_i, fin):
                par = 0 if dirn == "F" else 1
                P0, P1, P2, P3 = [4 * par + j for j in range(4)]
                col = i * 4 + h
                uX = gsc["u" + dirn][:, col:col + 1]
                wX = gsc["w" + dirn][:, col:col + 1]
                msk = maskFq if dirn == "F" else maskBq
                ts_ = slice(i * 128, (i + 1) * 128)
                rq = [R("qk", 0, i), R("qk", 1, i)]
                rk = [R("qk", 2, i), R("qk", 3, i)]
                RCh = R("Chat", dirn)
                RCb = R("Cbf", dirn)
                MMG([(banks[P0][:, 0:128], kT[:, j, ts_], qT[:, j, ts_], j == 0, j == 1) for j in range(2)],
                    rq + rk, [PB(P0)])
                TRG([(bank_bf(P0)[:, 256 + j * 128:256 + (j + 1) * 128], kT[:, j, ts_]) for j in range(2)],
                    rk + [RC], [PB(P0)])
                STT(PT[par], banks[P0][:, 0:128], uX, msk, ALU.mult, ALU.mult, [PB(P0), RGS, RC], [R("PT", par)])
                ACTF(kw[par], bank_bf(P0)[:, 256:512], AF.Copy, [PB(P0), RGS], [R("kw", par)], scale=uX)
                items = []
                if not first:
                    sp_ = gsc["s" + dirn][:, prev_i * 4 + h:prev_i * 4 + h + 1]
                    sp16 = gsc["s" + dirn + "16"][:, prev_i * 4 + h:prev_i * 4 + h + 1]
                    TS("pool", Cbf[dirn][:, 0, 0:257], Chat[dirn][:, 0, :], sp_, 1.0 / 16.0, ALU.mult, ALU.mult,
                       [RCh, R("Chn", dirn), RGS], [R("Cbf", dirn, 0)])
                    ACTF(Cbf[dirn][:, 1, 0:257], Chat[dirn][:, 1, :], AF.Copy, [RCh, R("Chn", dirn), RGS, R("gs16", 0), R("gs16", 1)], [R("Cbf", dirn, 1)], scale=sp16)
                    items += [(banks[P1][:, 0:257], qT[:, j, ts_], Cbf[dirn][:, j, 0:257], j == 0, False) for j in range(2)]
                items += [(banks[P1][:, 0:257], PT[par], vext[:, i, 0:257], first, True)]
                MMG(items, rq + [R("Cbf", dirn, 0), R("Cbf", dirn, 1), R("PT", par), R("v", i), R("vones")], [PB(P1)])
                MMG([(banks[P1][:, 260 + j:261 + j], kw[par][:, j * 128:(j + 1) * 128], vext[:, i, 256:257], True, True)
                     for j in range(2)], [R("kw", par), R("vones")], [PB(P1)])
                MMG([(banks[P2][:, j * 256:(j + 1) * 256], kw[par][:, j * 128:(j + 1) * 128], vext[:, i, 0:256], True, True)
                     for j in range(2)], [R("kw", par), R("v", i)], [PB(P2)])
                ACTF(nds[par], banks[P1][:, 0:257], AF.Copy, [PB(P1), RGS], [R("nds", par)], scale=wX)
                so = ((i % 16) * 2 + (0 if dirn == "F" else 1)) * 8
                dn = sml[:, so:so + 1]
                rd = sml[:, so + 1:so + 2]
                ACTF(dn, nds[par][:, 256:257], AF.Abs, [R("nds", par)], [R("dn", so)])
                TS("dve", rd, dn, 1.0, None, ALU.max, None, [R("dn", so)], [R("dn2", so)])
                E("dve", lambda e, o_=rd, i_=rd: e.reciprocal(o_, i_), [R("dn2", so)], [R("rd", so)], 0.2)
                dC3 = banks[P2][:, :].rearrange("p (a b) -> p a b", a=2, b=256)
                if first:
                    E("dve", lambda e, o_=Chat[dirn][:, :, 0:256], i_=dC3: e.tensor_copy(o_, i_), [PB(P2)], [RCh], 0.65)
                    E("dve", lambda e, o_=Chat[dirn][:, :, 256], i_=banks[P1][:, 260:262]: e.tensor_copy(o_, i_),
                      [PB(P1)], [R("Chn", dirn)], 0.15)
                else:
                    STT(Chat[dirn][:, :, 0:256], Chat[dirn][:, :, 0:256], sp_, dC3, ALU.mult, ALU.add,
                        [RCh, RGS, PB(P2)], [RCh])
                    STT(Chat[dirn][:, :, 256], Chat[dirn][:, :, 256], sp_, banks[P1][:, 260:262], ALU.mult, ALU.add,
                        [R("Chn", dirn), RGS, PB(P1)], [R("Chn", dirn)])
                if not fin:
                    k3 = 2 * par + i % 2
                    ACTF(hfo[k3], nds[par][:, 0:256], AF.Copy, [R("nds", par), R("rd", so)], [R("hfo", k3)], scale=rd)
                    DMA("sp", hf_d[ts_, :], hfo[k3], [R("hfo", k3)], [RHF(i)], hfo_sem[k3])
                    return
                k3 = 2 * par + i % 2
                MMG([(banks[P3][:, 0:512], hT[:, kc, ts_], wog[:, kc, :], kc == 0, kc == 7) for kc in range(8)],
                    [RH(i)] + RWOG, [PB(P3)])
                ACTF(sgo[par], banks[P3][:, :], AF.Sigmoid, [PB(P3)], [R("sgo", par)])
                DMA("sp", hfi[k3], hf_d[ts_, :], [RHF(i)], [R("hfi", k3)], hfi_sem[k3])
                STT(hs[par], nds[par][:, 0:256], rd, hfi[k3], ALU.mult, ALU.add,
                    [R("nds", par), R("rd", so), R("hfi", k3)], [R("hs", par)])
                sq = sml[:, so + 2:so + 3]
                ve = sml[:, so + 3:so + 4]
                rs = sml[:, so + 4:so + 5]
                ACTF(junk3, hs[par], AF.Square, [R("hs", par)], [R("junk3"), R("sq", so)], accum_out=sq)
                TS("dve", ve, sq, 1.0 / 256.0, EPS, ALU.mult, ALU.add, [R("sq", so)], [R("ve3", so)])
                RSTD(ve, rs, R("ve3", so), R("rs3", so))
                TT("pool", y2[par], sgo[par][:, 0:256], sgo[par][:, 256:512], ALU.mult, [R("sgo", par)], [R("y2", par)])
                TT("pool", y2[par], y2[par], mhs[:, hc], ALU.mult, [R("y2", par), RP3], [R("y2", par)])
                STT(mxo[k3], hs[par], rs, y2[par], ALU.mult, ALU.mult, [R("hs", par), R("rs3", so), R("y2", par)], [R("mxo", k3)])
                DMA("sp", mixa_d[ts_, hc], mxo[k3], [R("mxo", k3)], [R("mixa_d", i, h)], mxo_sem[k3])

            if h + 1 < 4:
                load_qkv(h + 1)
            for st_ in range(NT):
                sweep_tile(st_, "F", st_ == 0, st_ - 1, st_ >= NT // 2)
                sweep_tile(NT - 1 - st_, "B", st_ == 0, NT - st_, st_ >= NT // 2)
            SCH.barrier()
            if stop_after <= 2.8:
                return finish(nc, SCH, esem, None)
        ptr[0] = 0
        if stop_after <= 3:
            return finish(nc, SCH, esem, None)

        ptr[0] = base_persist - 8 * S * 2
        w1s = alloc([128, 8, 4 * D], BF16)
        w2s = alloc([128, 32, D], BF16)
        wos = alloc([128, 8, D], BF16)
        wosem = new_dsem()
        wfsem = new_dsem()
        RWO = R("wo")
        RWF = R("wf")
        DMA("pool", wos, w_out.rearrange("(kc p) n -> p kc n", p=128), [], [R("wos")], None)
        RW1 = [R("w1s", 0), R("w1s", 1)]
        RW2 = [R("w2s", 0), R("w2s", 1)]
        for c in range(2):
            DMA("pool", w1s[:, :, c * 2048:(c + 1) * 2048],
                w1[:, c * 2048:(c + 1) * 2048].rearrange("(kc p) n -> p kc n", p=128), [], [RW1[c]], None)
        for c in range(2):
            DMA("pool", w2s[:, c * 16:(c + 1) * 16, :],
                w2[c * 2048:(c + 1) * 2048, :].rearrange("(kc p) n -> p kc n", p=128), [], [RW2[c]], None)
        g1s = alloc([128, D])
        g2s = alloc([128, D])
        nfs = alloc([128, D])
        A2p = alloc([128, 8])
        B2p = alloc([128, 8])
        DMA("sp", g1s, mod_d[2048:3072].partition_broadcast(128), [RMOD], [RWO], wosem)
        DMA("sp", g2s, mod_d[5120:6144].partition_broadcast(128), [RMOD], [RWF], wfsem)
        DMA("sp", nfs, nfgb, [], [RWF], wfsem)
        DMA("sp", A2p, mod_d[4096:5120].rearrange("(c p) -> p c", p=128), [RMOD], [RWF], wfsem,
            allow_slow_non_contiguous=True)
        DMA("sp", B2p, mod_d[3072:4096].rearrange("(c p) -> p c", p=128), [RMOD], [RWF], wfsem,
            allow_slow_non_contiguous=True)
        STT(A2p, A2p, 1.0, n2gs, ALU.add, ALU.mult, [RWF, RC], [R("A2p")])
        base_p4 = ptr[0]
        mxi = [alloc([128, D], BF16) for _ in range(3)]
        mxi_sem = [new_dsem() for _ in range(3)]
        xi = [alloc([128, D]) for _ in range(3)]
        xi_sem = [new_dsem() for _ in range(3)]
        mT = [alloc([128, 8, 128], BF16) for _ in range(2)]
        tpj = [alloc([128, D]) for _ in range(2)]
        x1o = [alloc([128, D]) for _ in range(2)]
        x1o_sem = [new_dsem() for _ in range(2)]
        mai = [alloc([128, D], BF16) for _ in range(3)]
        mai_sem = [new_dsem() for _ in range(3)]

        def RX1(i):
            return R("x1_d", i)

        for i in range(NT):
            k = i % 2
            k3 = i % 3
            ts_ = slice(i * 128, (i + 1) * 128)
            DMA("sp", mxi[k3], mix_d[ts_, :], [RM(i)], [R("mxi", k3)], mxi_sem[k3])
            DMA("sp", xi[k3], x[ts_, :], [], [R("xi", k3)], xi_sem[k3])
            DMA("sp", mai[k3], mixa_d[ts_, :], [R("mixa_d", i, hh_) for hh_ in range(4)], [R("mai", k3)], mai_sem[k3])
            TT("dve", mxi[k3], mxi[k3], mai[k3], ALU.add, [R("mxi", k3), R("mai", k3)], [R("mxi", k3)])
            TRG([(bank_bf(k)[:, kc * 128:(kc + 1) * 128], mxi[k3][:, kc * 128:(kc + 1) * 128]) for kc in range(8)],
                [R("mxi", k3), RC], [PB(k)])
            ACTF(mT[k], bank_bf(k).rearrange("p (a b) -> p a b", a=8, b=128), AF.Copy, [PB(k)], [R("mT", k)])
            for hh in range(2):
                bk = 2 + 2 * k + hh
                cs = slice(hh * 512, (hh + 1) * 512)
                MMG([(banks[bk][:, 0:512], mT[k][:, kc, :], wos[:, kc, cs], kc == 0, kc == 7) for kc in range(8)],
                    [R("mT", k), R("wos")], [PB(bk)])
                TT("dve", tpj[k][:, cs], banks[bk][:, :], g1s[:, cs], ALU.mult, [PB(bk), RWO], [R("tpj", k, hh)])
            TT("pool", x1o[k], tpj[k], xi[k3], ALU.add, [R("tpj", k, 0), R("tpj", k, 1), R("xi", k3)], [R("x1o", k)])
            DMA("sp", x1_d[ts_, :], x1o[k], [R("x1o", k)], [RX1(i)], x1o_sem[k])
        SCH.barrier()
        ptr[0] = base_p4
        if stop_after <= 4:
            return finish(nc, SCH, esem, None)

        xg = [alloc([128, 2, D]) for _ in range(2)]
        xg_sem = [[new_dsem() for _ in range(2)] for _ in range(2)]
        h2t = [alloc([128, D], BF16) for _ in range(2)]
        h2T = [alloc([128, 8, 256], BF16) for _ in range(2)]
        sq4 = [alloc([128, 256]) for _ in range(4)]
        fT = [alloc([128, 256], BF16) for _ in range(4)]
        tf = [alloc([128, D]) for _ in range(2)]
        ob_sem = [new_dsem() for _ in range(2)]
        junk4 = alloc([128, D], BF16)
        sm4 = alloc([128, 256])
        out_ops = []
        NG = S // 256
        for g in range(NG):
            kg = g % 2
            for s_ in range(2):
                i = 2 * g + s_
                ts_ = slice(i * 128, (i + 1) * 128)
                so = (i % 32) * 8
                DMA("sp", xg[kg][:, s_, :], x1_d[ts_, :], [RX1(i)], [R("xg", kg, s_)], xg_sem[kg][s_])
                ACTF(h2t[s_], xg[kg][:, s_, :], AF.Square, [R("xg", kg, s_)], [R("h2t", s_), R("s4", so)],
                     accum_out=sm4[:, so:so + 1])
                TS("dve", sm4[:, so + 1:so + 2], sm4[:, so:so + 1], 1.0 / D, EPS, ALU.mult, ALU.add, [R("s4", so)], [R("v4", so)])
                RSTD(sm4[:, so + 1:so + 2], sm4[:, so + 2:so + 3], R("v4", so), R("r4", so))
                TS("dve", h2t[s_], xg[kg][:, s_, :], sm4[:, so + 2:so + 3], None, ALU.mult, None,
                   [R("xg", kg, s_), R("r4", so)], [R("h2t", s_)])
                TRG([(bank_bf(3)[:, kc * 128:(kc + 1) * 128], h2t[s_][:, kc * 128:(kc + 1) * 128]) for kc in range(8)],
                    [R("h2t", s_), RC], [PB(3)])
                for kc in range(8):
                    ACTF(h2T[kg][:, kc, s_ * 128:(s_ + 1) * 128], bank_bf(3)[:, kc * 128:(kc + 1) * 128], AF.Identity,
                         [PB(3), R("A2p"), RWF], [R("h2T", kg, s_)], scale=A2p[:, kc:kc + 1], bias=B2p[:, kc:kc + 1])
            rh2 = [R("h2T", kg, 0), R("h2T", kg, 1)]

            def ffn1(ft):
                bk = ft % 3
                MMG([(banks[bk][:, 0:256], w1s[:, kc, ft * 128:(ft + 1) * 128], h2T[kg][:, kc, :], kc == 0, kc == 7)
                     for kc in range(8)], rh2 + RW1, [PB(bk)])

            ffn1(0)
            for ft in range(32):
                bk = ft % 4
                pb = ft % 3
                if ft + 1 < 32:
                    ffn1(ft + 1)
                ACTF(sq4[bk], banks[pb][:, 0:256], AF.Square, [PB(pb)], [R("sq4", bk)])
                STT(fT[bk], banks[pb][:, 0:256], 0.0, sq4[bk], ALU.is_gt, ALU.mult, [PB(pb), R("sq4", bk)], [R("fT", bk)])
                for s_ in range(2):
                    for hh in range(2):
                        ob_ = 4 + s_ * 2 + hh
                        MMG([(banks[ob_][:, 0:512], fT[bk][:, s_ * 128:(s_ + 1) * 128], w2s[:, ft, hh * 512:(hh + 1) * 512],
                              ft == 0, ft == 31)], [R("fT", bk)] + RW2, [PB(ob_)])
            for s_ in range(2):
                i = 2 * g + s_
                ts_ = slice(i * 128, (i + 1) * 128)
                so = (i % 32) * 8
                for hh in range(2):
                    ob_ = 4 + s_ * 2 + hh
                    cs = slice(hh * 512, (hh + 1) * 512)
                    TT("dve", tf[s_][:, cs], banks[ob_][:, :], g2s[:, cs], ALU.mult, [PB(ob_), RWF], [R("tf", s_, hh)])
                TT("pool", xg[kg][:, s_, :], tf[s_], xg[kg][:, s_, :], ALU.add,
                   [R("tf", s_, 0), R("tf", s_, 1), R("xg", kg, s_)], [R("xg", kg, s_)])
                ACTF(junk4, xg[kg][:, s_, :], AF.Square, [R("xg", kg, s_)], [R("junk4"), R("s5", so)],
                     accum_out=sm4[:, so + 3:so + 4])
                TS("dve", sm4[:, so + 4:so + 5], sm4[:, so + 3:so + 4], 1.0 / D, EPS, ALU.mult, ALU.add, [R("s5", so)], [R("v5", so)])
                RSTD(sm4[:, so + 4:so + 5], sm4[:, so + 5:so + 6], R("v5", so), R("r5", so))
                STT(tf[s_], xg[kg][:, s_, :], sm4[:, so + 5:so + 6], nfs, ALU.mult, ALU.mult,
                    [R("xg", kg, s_), R("r5", so), RWF], [R("tf", s_, 0), R("tf", s_, 1)])
                out_ops.append(DMA("sp", out[ts_, :], tf[s_], [R("tf", s_, 0), R("tf", s_, 1)], [R("out", i)], ob_sem[s_]))
        return finish(nc, SCH, esem, out_ops)


def finish(nc, SCH, esem, out_ops):
    order = SCH.schedule()

    def replay(e, h):
        for o in order[e]:
            for d in o.waits:
                if d.dsem is not None:
                    h.wait_ge(d.dsem.handle, d.dval)
                else:
                    h.wait_ge(esem[d.eng], d.count)
            if o.fn is None:
                if o.signal:
                    h.nop().then_inc(esem[e], 1)
                continue
            ins = o.fn(h)
            if o.dsem is not None:
                ins.then_inc(o.dsem.handle, 16)
            elif o.signal:
                ins.then_inc(esem[e], 1)

    with nc.Block() as block:
        @block.tensor
        def _(h):
            replay("pe", h)

        @block.scalar
        def _(h):
            replay("act", h)

        @block.vector
        def _(h):
            replay("dve", h)

        @block.gpsimd
        def _(h):
            replay("pool", h)

        @block.sync
        def _(h):
            replay("sp", h)
    return nc


def host_inputs(b, x, c, w_ada, b_ada, norm1_g, norm2_g, w_in, b_if, conv_w, conv_b, mh_g,
                ln_v_g, ln_v_b, w_s, b_s, w_out, w1, w2, normf_g):
    f = np.float32

    def bc(v):
        return np.ascontiguousarray(np.broadcast_to(np.asarray(v, f).reshape(1, -1), (128, v.size)))

    r = np.arange(128)
    mF = (r[:, None] <= r[None, :]).astype(f)
    mB = (r[:, None] >= r[None, :]).astype(f)
    cm = np.concatenate([mF, mB, np.ones((128, 128), f), mF / 16.0, mB / 16.0], axis=1)
    cw = np.asarray(conv_w[0], f)
    cwp = np.ascontiguousarray(cw.reshape(3, 16, 128).transpose(2, 1, 0).reshape(128, 48))
    cbp = np.ascontiguousarray(np.asarray(conv_b[0], f).reshape(16, 128).T)
    return {
        "x": np.ascontiguousarray(x[b], dtype=f),
        "c_b": np.ascontiguousarray(np.asarray(c[b], f).reshape(8, 128).T),
        "w_ada": np.ascontiguousarray(w_ada[0], dtype=f),
        "b_ada": np.ascontiguousarray(b_ada[0:1], dtype=f),
        "n1gb": bc(norm1_g[0]),
        "n2gp": np.ascontiguousarray(np.asarray(norm2_g[0], f).reshape(8, 128).T),
        "w_in": np.ascontiguousarray(w_in[0], dtype=f),
        "bifb": np.ascontiguousarray(np.broadcast_to(np.asarray(b_if[0], f).reshape(1, 1, 16), (128, 32, 16)).reshape(128, 512)),
        "cwp": cwp,
        "cbp": cbp,
        "mhgb": bc(mh_g[0]),
        "lngb": bc(ln_v_g[0]),
        "lnbb": bc(ln_v_b[0]),
        "wsT": np.ascontiguousarray(np.asarray(w_s[0], f).transpose(2, 0, 1).reshape(128, 1024)),
        "bsp": np.ascontiguousarray(np.asarray(b_s[0], f).T),
        "w_out": np.ascontiguousarray(w_out[0], dtype=f),
        "w1": np.ascontiguousarray(w1[0], dtype=f),
        "w2": np.ascontiguousarray(w2[0], dtype=f),
        "nfgb": bc(normf_g),
        "cmask": cm,
        "identb": np.eye(128, dtype=f).astype(ml_dtypes.bfloat16),
    }


def kernel(**inputs):
    inputs = {k: np.asarray(v) for k, v in inputs.items()}
    nc = build_program()
    in_maps = [host_inputs(b, **inputs) for b in range(8)]
    res = run_bass_kernel_spmd(nc, in_maps, core_ids=list(range(8)))
    return np.stack([np.asarray(r["out"], dtype=np.float32) for r in res.results], axis=0)
```

```python
import numpy as np
import ml_dtypes
from contextlib import ExitStack
import concourse.bass as bass
import concourse.mybir as mybir
from concourse.bass_utils import run_bass_kernel_spmd

F32 = mybir.dt.float32
BF16 = mybir.dt.bfloat16
ALU = mybir.AluOpType
AF = mybir.ActivationFunctionType

D = 1024
S = 4096
NT = 32
DIN = 8208
EPS = 1e-6
ARENA_WORDS = 52992
ENGS = ["pe", "act", "dve", "pool", "sp"]


class Res:
    __slots__ = ("name", "lw", "rd")

    def __init__(self, name):
        self.name = name
        self.lw = None
        self.rd = []


class DSem:
    __slots__ = ("handle", "n", "last")

    def __init__(self, handle):
        self.handle = handle
        self.n = 0
        self.last = None


class Op:
    __slots__ = ("eng", "fn", "deps", "gid", "seg", "signal", "dsem", "dval", "count", "occ", "lat",
                 "pos", "fin", "nrem", "succ", "rdy", "waits", "is_bar", "line", "soft", "_excl")


SYNC_LAT = 0.35


class Sched:
    def __init__(self):
        self.all = []
        self.res = {}
        self.seg = 0

    def R(self, *key):
        r = self.res.get(key)
        if r is None:
            r = Res(key)
            self.res[key] = r
        return r

    def op(self, eng, fn, reads=(), writes=(), dsem=None, occ=0.3, lat=None):
        o = Op()
        o.eng = eng
        o.fn = fn
        o.gid = len(self.all)
        o.seg = self.seg
        o.signal = False
        o.dsem = dsem
        o.dval = 0
        o.count = 0
        o.occ = occ
        o.lat = occ if lat is None else lat
        o.is_bar = False
        import sys as _sys
        f_ = _sys._getframe(1)
        while f_.f_code.co_name != "build_program" and f_.f_back is not None and f_.f_code.co_name not in ("sweep_tile", "p2_proj", "ffn1"):
            f_ = f_.f_back
        o.line = f_.f_lineno
        deps = {}
        soft = set()
        hard = set()

        def add(d, is_soft=False):
            if d is not None and d.seg == o.seg:
                deps[d.gid] = d
                (soft if is_soft else hard).add(d.gid)

        excl = [r for r in reads if r.name[0] == "bank" and r not in writes]
        if excl:
            reads = [r for r in reads if r.name[0] != "bank"]
        for r in reads:
            add(r.lw)
        for w in writes:
            add(w.lw)
            for rr in w.rd:
                add(rr)
        for w in excl:
            lw = w.lw
            if lw is not None:
                add(lw, is_soft=(lw.eng == eng and w in getattr(lw, "_excl", ())))
            for rr in w.rd:
                add(rr)
        o._excl = excl
        writes = list(writes) + excl
        if dsem is not None:
            add(dsem.last, is_soft=True)
            dsem.n += 1
            o.dval = 16 * dsem.n
            dsem.last = o
        o.soft = soft - hard
        o.deps = list(deps.values())
        for r in reads:
            if r.rd and r.rd[-1].seg != o.seg:
                r.rd = []
            r.rd.append(o)
        for w in writes:
            w.lw = o
            w.rd = []
        self.all.append(o)
        return o

    def barrier(self):
        self.seg += 1

    def schedule(self):
        import heapq
        nseg = self.seg + 1
        segs = [[] for _ in range(nseg)]
        for o in self.all:
            segs[o.seg].append(o)
        order = {e: [] for e in ENGS}
        for si, ops in enumerate(segs):
            for o in ops:
                o.succ = []
                o.nrem = len(o.deps)
                o.rdy = 0.0
            for o in ops:
                for d in o.deps:
                    d.succ.append(o)
            ready = {e: [] for e in ENGS}
            for o in ops:
                if o.nrem == 0:
                    heapq.heappush(ready[o.eng], (0.0, o.gid, o))
            free = {e: 0.0 for e in ENGS}
            seg_order = {e: [] for e in ENGS}
            left = len(ops)
            while left:
                best = None
                for e in ENGS:
                    if ready[e]:
                        rt, gid, o = ready[e][0]
                        st = max(rt, free[e])
                        if best is None or (st, gid) < (best[0], best[1]):
                            best = (st, gid, o)
                st, gid, o = best
                heapq.heappop(ready[o.eng])
                free[o.eng] = st + o.occ
                o.fin = st + o.lat
                seg_order[o.eng].append(o)
                left -= 1
                for sc in o.succ:
                    lat = 0.0 if (sc.eng == o.eng and o.dsem is None) else SYNC_LAT
                    sc.rdy = max(sc.rdy, o.fin + lat)
                    sc.nrem -= 1
                    if sc.nrem == 0:
                        heapq.heappush(ready[sc.eng], (sc.rdy, sc.gid, sc))
            last_dma = {}
            for o in ops:
                if o.dsem is not None:
                    last_dma[o.dsem] = o
            bar = Op()
            bar.eng = "sp"; bar.fn = None; bar.gid = -1; bar.seg = si; bar.signal = False
            bar.dsem = None; bar.dval = 0; bar.count = 0; bar.is_bar = True
            bar.deps = [seg_order[e][-1] for e in ENGS if e != "sp" and seg_order[e]] + list(last_dma.values())
            seg_order["sp"].append(bar)
            if si + 1 < nseg:
                for e in ENGS:
                    if e != "sp":
                        b2 = Op()
                        b2.eng = e; b2.fn = None; b2.gid = -1; b2.seg = si; b2.signal = False
                        b2.dsem = None; b2.dval = 0; b2.count = 0; b2.is_bar = True
                        b2.deps = [bar]
                        seg_order[e].append(b2)
            for e in ENGS:
                order[e] += seg_order[e]
        for e in ENGS:
            for p, o in enumerate(order[e]):
                o.pos = p
        for e in ENGS:
            seen = {}
            for o in order[e]:
                need = {}
                for d in o.deps:
                    if d.gid in getattr(o, "soft", ()):
                        continue
                    if d.dsem is not None:
                        key, val = d.dsem, d.dval
                    else:
                        if d.eng == e and (e == "pe" or e == "sp" or d.gid in getattr(o, "soft", ())):
                            continue
                        key, val = d.eng, d.pos
                    if seen.get(key, -1) >= val:
                        continue
                    if key not in need or need[key][0] < val:
                        need[key] = (val, d)
                o.waits = []
                for key, (val, d) in need.items():
                    seen[key] = val
                    if d.dsem is None:
                        d.signal = True
                    o.waits.append(d)
        for e in ENGS:
            c = 0
            for o in order[e]:
                if o.signal and o.dsem is None:
                    c += 1
                    o.count = c
        return order


def build_program(debug=False, stop_after=99):
    nc = bass.Bass("TRN2", target_bir_lowering=False)
    SCH = Sched()
    R = SCH.R

    def din(name, shape, dt=F32):
        return nc.dram_tensor(name, list(shape), dt, kind="ExternalInput").ap()

    x = din("x", [S, D])
    c_b = din("c_b", [128, 8])
    w_ada = din("w_ada", [D, 6 * D])
    b_ada = din("b_ada", [1, 6 * D])
    n1gb = din("n1gb", [128, D])
    n2gp = din("n2gp", [128, 8])
    w_in = din("w_in", [D, DIN])
    bifb = din("bifb", [128, 512])
    cwp = din("cwp", [128, 48])
    cbp = din("cbp", [128, 16])
    mhgb = din("mhgb", [128, D])
    lngb = din("lngb", [128, D])
    lnbb = din("lnbb", [128, D])
    wsT = din("wsT", [128, 1024])
    bsp = din("bsp", [128, 8])
    w_out = din("w_out", [D, D])
    w1 = din("w1", [D, 4 * D])
    w2 = din("w2", [4 * D, D])
    nfgb = din("nfgb", [128, D])
    cmask = din("cmask", [128, 640])
    identb = din("identb", [128, 128], BF16)
    out = nc.dram_tensor("out", [S, D], F32, kind="ExternalOutput").ap()
    sk = "ExternalOutput" if debug else "Internal"
    mod_d = nc.dram_tensor("mod_d", [6 * D], F32, kind=sk).ap()
    mix_d = nc.dram_tensor("mix_d", [S, D], BF16, kind=sk).ap()
    hf_d = nc.dram_tensor("hf_d", [S, 256], F32, kind=sk).ap()
    mixa_d = nc.dram_tensor("mixa_d", [S, D], BF16, kind=sk).ap()
    x1_d = nc.dram_tensor("x1_d", [S, D], F32, kind=sk).ap()

    es = ExitStack()
    with es:
        arena = es.enter_context(nc.sbuf_tensor("arena", [128, ARENA_WORDS], F32))
        banks = [es.enter_context(nc.psum_tensor(f"ps{i}", [128, 512], F32)) for i in range(8)]
        esem = {e: es.enter_context(nc.semaphore(f"sem_{e}")) for e in ENGS}
        nsem = [0]

        def new_dsem():
            nsem[0] += 1
            return DSem(es.enter_context(nc.semaphore(f"dsem{nsem[0]}")))

        ptr = [0]

        def alloc(shape, dt=F32):
            n = 1
            for s_ in shape[1:]:
                n *= s_
            nb = n * (4 if dt == F32 else 2)
            nb = (nb + 31) // 32 * 32
            off = ptr[0]
            ptr[0] += nb
            assert (ptr[0] <= top[0] or off >= top[0]) and ptr[0] <= ARENA_WORDS * 4, ("arena overflow", ptr[0], top[0])
            ap = arena[:, off // 4:(off + nb) // 4]
            if dt != F32:
                ap = ap.bitcast(dt)
            ap = ap[:, 0:n]
            if len(shape) == 3:
                ap = ap.rearrange("p (a b) -> p a b", a=shape[1], b=shape[2])
            return ap

        top = [ARENA_WORDS * 4]

        def alloc_top(shape, dt=F32):
            n = 1
            for s_ in shape[1:]:
                n *= s_
            nb = (n * (4 if dt == F32 else 2) + 31) // 32 * 32
            top[0] -= nb
            save = ptr[0]
            ptr[0] = top[0]
            ap = alloc(shape, dt)
            ptr[0] = save
            return ap

        def bank_bf(i):
            return banks[i][:, :].bitcast(BF16)

        def PB(i):
            return R("bank", i)

        def nfree(ap):
            n = 1
            for s_ in ap.shape[1:]:
                n *= s_
            return n

        def E(eng, fn, reads=(), writes=(), occ=0.3):
            return SCH.op(eng, fn, reads, writes, occ=occ)

        def DMA(q, o_ap, i_ap, reads, writes, sem, **kw):
            if q == "pool":
                sem = new_dsem()
            nbytes = nfree(o_ap) * o_ap.shape[0] * (4 if o_ap.dtype == F32 else 2)
            return SCH.op(q, lambda e, o_ap=o_ap, i_ap=i_ap, kw=kw: e.dma_start(out=o_ap, in_=i_ap, **kw),
                          reads, writes, dsem=sem, occ=0.4 if q == "sp" else 1.5, lat=3.0 + nbytes / 120e3)

        def MMG(items, reads, writes):
            def fn(e, items=items):
                ins = None
                for (o_, l_, r_, st, sp) in items:
                    ins = e.matmul(o_, l_, r_, start=st, stop=sp)
                return ins
            t = 0.0
            for (o_, l_, r_, st, sp) in items:
                t += max(64, nfree(r_)) * 0.00043 * (4 if l_.dtype == F32 else 1) + 0.01
            return SCH.op("pe", fn, reads, writes, occ=t, lat=t + 0.15)

        def TRG(items, reads, writes):
            def fn(e, items=items):
                ins = None
                for (o_, i_) in items:
                    ins = e.transpose(o_, i_, ident)
                return ins
            t = 0.11 * len(items)
            return SCH.op("pe", fn, reads, writes, occ=t, lat=t + 0.15)

        def ACTF(o_, i_, func, reads, writes, eng="act", **kw):
            t = 0.22 + nfree(o_) * 0.00085 + (0.1 if "accum_out" in kw else 0.0)
            return SCH.op("act", lambda e, o_=o_, i_=i_, func=func, kw=kw: e.activation(o_, i_, func, **kw),
                          reads, writes, occ=t)

        def vcost(eng, o_):
            return (0.12 + nfree(o_) * 0.00104) if eng == "dve" else (0.15 + nfree(o_) * 0.0022)

        def TS(eng, o_, i_, s1, s2, op0, op1, reads, writes):
            if op1 is None:
                return E(eng, lambda e, o_=o_, i_=i_: e.tensor_scalar(o_, i_, s1, None, op0), reads, writes, vcost(eng, o_))
            return E(eng, lambda e, o_=o_, i_=i_: e.tensor_scalar(o_, i_, s1, s2, op0, op1), reads, writes, vcost(eng, o_))

        def TT(eng, o_, a_, b_, op, reads, writes):
            return E(eng, lambda e, o_=o_, a_=a_, b_=b_, op=op: e.tensor_tensor(o_, a_, b_, op), reads, writes,
                     vcost(eng, o_))

        def STT(o_, a_, sc, b_, op0, op1, reads, writes):
            return E("dve", lambda e, o_=o_, a_=a_, sc=sc, b_=b_, op0=op0, op1=op1:
                     e.scalar_tensor_tensor(o_, a_, sc, b_, op0, op1), reads, writes, vcost("dve", o_))

        def RSTD(ve_ap, out_ap, r_in, r_out):
            return TT("pool", out_ap, ve_ap, neghalf[:, 0:1], ALU.pow, [r_in, RC], [r_out])

        csem = new_dsem()
        RC = R("consts")
        ident = alloc([128, 128], BF16)
        cm = alloc([128, 640])
        maskF = cm[:, 0:128]
        maskB = cm[:, 128:256]
        ones = cm[:, 256:384]
        maskFq = cm[:, 384:512]
        maskBq = cm[:, 512:640]
        neghalf = alloc([128, 8])
        cB = alloc([128, 8])
        bifs = alloc([128, 512])
        cws = alloc([128, 48])
        cbs = alloc([128, 16])
        bss = alloc([128, 8])
        n2gs = alloc([128, 8])
        for (dst, src) in [(ident, identb), (cm, cmask), (cB, c_b), (bifs, bifb), (cws, cwp), (cbs, cbp),
                           (bss, bsp), (n2gs, n2gp)]:
            DMA("sp", dst, src, [], [RC], csem)
        E("pool", lambda e: e.memset(neghalf, -0.5), [], [RC])
        hT = alloc([128, 8, S], BF16)
        base_persist = ptr[0]

        def RH(i):
            return R("hT", i)

        cact = alloc([128, 8])
        csig = alloc([128, 8])
        bada_s = alloc([1, 6 * D])
        NW0 = 4
        rowb = [alloc([1, 512]) for _ in range(NW0)]
        wst = [alloc([128, 8, 512]) for _ in range(NW0)]
        wst_sem = [new_dsem() for _ in range(NW0)]
        row_sem = [new_dsem() for _ in range(NW0)]
        DMA("sp", bada_s[0:1, :], b_ada[0:1, :], [], [RC], csem)
        ACTF(cact, cB, AF.Silu, [RC], [R("cact")])
        RMOD = R("mod_d")
        for nb in range(12):
            b = nb % NW0
            DMA("sp", wst[b], w_ada[:, nb * 512:(nb + 1) * 512].rearrange("(kc p) n -> p kc n", p=128),
                [], [R("wst", b)], wst_sem[b])
            MMG([(banks[b][0:1, 0:512], cact[:, kc:kc + 1], wst[b][:, kc, :], kc == 0, kc == 7) for kc in range(8)],
                [R("cact"), R("wst", b)], [PB(b)])
            TT("dve", rowb[b][0:1, :], banks[b][0:1, 0:512], bada_s[0:1, nb * 512:(nb + 1) * 512], ALU.add,
               [PB(b), RC], [R("rowb", b)])
            DMA("sp", mod_d[nb * 512:(nb + 1) * 512].rearrange("(o n) -> o n", o=1), rowb[b][0:1, :],
                [R("rowb", b)], [RMOD], row_sem[b])
        SCH.barrier()
        ptr[0] = base_persist

        wsgu = alloc([128, 8, 3072], BF16)
        wsgu_sem = new_dsem()
        RW = R("lnc")
        RWs = [R("wsgu", 0), R("wsgu", 1)]
        wss = alloc([128, 8, 128], BF16)

        def p2_weight_loads(after):
            DMA("pool", wsgu[:, :, 0:2048], w_in[:, 4096:6144].rearrange("(kc p) n -> p kc n", p=128), after, [RWs[0]], None)
            DMA("pool", wsgu[:, :, 2048:3072], w_in[:, 7168:8192].rearrange("(kc p) n -> p kc n", p=128), after, [RWs[1]], None)
            DMA("pool", wss, wsT.rearrange("p (g q) -> p g q", g=8), after, [R("wss")], None)
        lgs = alloc([128, D])
        lbs = alloc([128, D])
        DMA("sp", lgs, lngb, [], [RW], wsgu_sem)
        DMA("sp", lbs, lnbb, [], [RW], wsgu_sem)
        base_p2w = ptr[0]
        A1b = alloc([128, D])
        B1b = alloc([128, D])
        tln = alloc([128, D])
        n1s = tln
        p1sem = new_dsem()
        DMA("sp", n1s, n1gb, [], [R("p1c")], p1sem)
        DMA("sp", A1b, mod_d[1024:2048].partition_broadcast(128), [RMOD], [R("p1c")], p1sem)
        DMA("sp", B1b, mod_d[0:1024].partition_broadcast(128), [RMOD], [R("p1c")], p1sem)
        STT(A1b, A1b, 1.0, n1s, ALU.add, ALU.mult, [R("p1c")], [R("A1b")])
        NXB = 2
        xb = [alloc([128, D]) for _ in range(NXB)]
        xsem = [new_dsem() for _ in range(NXB)]
        junk = alloc([128, D], BF16)
        ss = alloc([128, 64])
        t1 = [alloc([128, D])]
        xh = [alloc([128, D], BF16)] * 2

        def phase1_tile(i):
            k3, k2 = i % NXB, i % 2
            DMA("sp", xb[k3], x[i * 128:(i + 1) * 128, :], [], [R("xb", k3)], xsem[k3])
            ACTF(junk, xb[k3], AF.Square, [R("xb", k3)], [R("junk"), R("ss", i)], accum_out=ss[:, i:i + 1])
            TS("dve", ss[:, 32 + i:33 + i], ss[:, i:i + 1], 1.0 / D, EPS, ALU.mult, ALU.add, [R("ss", i)], [R("ve", i)])
            RSTD(ss[:, 32 + i:33 + i], ss[:, i:i + 1], R("ve", i), R("rs", i))
            STT(t1[0], xb[k3], ss[:, i:i + 1], A1b, ALU.mult, ALU.mult, [R("xb", k3), R("rs", i), R("A1b")], [R("t1", 0)])
            TT("dve", xh[k2], t1[0], B1b, ALU.add, [R("t1", 0), R("p1c")], [R("xh", 0)])
            TRG([(bank_bf(7)[:, kc * 128:(kc + 1) * 128], xh[k2][:, kc * 128:(kc + 1) * 128]) for kc in range(8)],
                [R("xh", 0), RC], [PB(7)])
            ACTF(hT[:, :, i * 128:(i + 1) * 128], bank_bf(7).rearrange("p (a b) -> p a b", a=8, b=128), AF.Copy,
                 [PB(7)], [RH(i)])

        wg = alloc_top([128, 8, 16], BF16)
        mhs = alloc_top([128, D])
        wqk_top = alloc_top([128, 8, 512], BF16)
        wv_top = alloc_top([128, 8, 256], BF16)
        p3sem = new_dsem()
        RP3 = R("p3c")
        DMA("pool", wg, w_in[:, 8192:8208].rearrange("(kc p) n -> p kc n", p=128), [], [R("wg")], None)
        DMA("sp", mhs, mhgb, [], [RP3], p3sem)
        for wi, (dst, c0) in enumerate([(wqk_top[:, :, 0:256], 0), (wqk_top[:, :, 256:512], 1024), (wv_top, 2048)]):
            DMA("pool", dst, w_in[:, c0:c0 + 256].rearrange("(kc p) n -> p kc n", p=128), [], [R("wh", 0, wi)], None)
        sg = [alloc([128, D])]
        gu = [alloc([128, D]) for _ in range(2)]
        gv = [alloc([128, D]) for _ in range(2)]
        vn = [alloc([128, D], BF16)] * 2
        yb = alloc([128, D])
        mxb = [alloc([128, D], BF16)] * 2
        mxb_sem = [new_dsem()] * 2
        st2 = alloc([128, 32 * 16])

        def RM(i):
            return R("mix_d", i)

        def p2_proj(i):
            for g in range(6):
                MMG([(banks[g][:, 0:512], hT[:, kc, i * 128:(i + 1) * 128], wsgu[:, kc, g * 512:(g + 1) * 512],
                      kc == 0, kc == 7) for kc in range(8)], [RH(i)] + RWs, [PB(g)])

        phase1_tile(0)
        phase1_tile(1)
        p2_weight_loads([R("ss", 1)])
        p2_proj(0)
        for i in range(NT):
            k = i % 2
            sb = i * 16
            if i + 2 < NT:
                phase1_tile(i + 2)
            def act_u():
                for hh in range(2):
                    cs = slice(hh * 512, (hh + 1) * 512)
                    ACTF(gu[k][:, cs], banks[hh][:, :], AF.Gelu, [PB(hh)], [R("gu", k, hh)])

            def act_v():
                for hh in range(2):
                    cs = slice(hh * 512, (hh + 1) * 512)
                    ACTF(gv[k][:, cs], banks[2 + hh][:, :], AF.Gelu, [PB(2 + hh)], [R("gv", k, hh)])
                    E("dve", lambda e, o_=st2[:, sb + hh * 6:sb + hh * 6 + 6], i_=gv[k][:, cs]: e.bn_stats(o_, i_),
                      [R("gv", k, hh)], [R("st2", i, hh)])

            def act_m():
                for hh in range(2):
                    cs = slice(hh * 512, (hh + 1) * 512)
                    ACTF(sg[0][:, cs], banks[4 + hh][:, :], AF.Sigmoid, [PB(4 + hh)], [R("sg", 0, hh)])

            if i % 2 == 0:
                act_u(); act_v(); act_m()
            else:
                act_m(); act_v(); act_u()
            if i + 1 < NT:
                p2_proj(i + 1)
            E("dve", lambda e, o_=st2[:, sb + 12:sb + 14], i_=st2[:, sb:sb + 12]: e.bn_aggr(o_, i_),
              [R("st2", i, 0), R("st2", i, 1)], [R("mv", i)])
            TS("dve", st2[:, sb + 14:sb + 15], st2[:, sb + 13:sb + 14], EPS, None, ALU.add, None,
               [R("mv", i)], [R("ve2", i)])
            RSTD(st2[:, sb + 14:sb + 15], st2[:, sb + 15:sb + 16], R("ve2", i), R("rs2", i))
            STT(tln, gv[k], st2[:, sb + 12:sb + 13], lgs, ALU.subtract, ALU.mult,
                [R("gv", k, 0), R("gv", k, 1), R("mv", i), RW], [R("tln")])
            STT(vn[k], tln, st2[:, sb + 15:sb + 16], lbs, ALU.mult, ALU.add, [R("tln"), R("rs2", i), RW], [R("vn", 0)])
            for rnd in range(2):
                bk = 6
                MMG([(banks[bk][:, (g8 % 4) * 128:(g8 % 4 + 1) * 128], wss[:, g8, :], vn[k][:, g8 * 128:(g8 + 1) * 128],
                      True, True) for g8 in range(4 * rnd, 4 * rnd + 4)], [R("vn", 0), R("wss")], [PB(bk)])
                for g8 in range(4 * rnd, 4 * rnd + 4):
                    gs = slice(g8 * 128, (g8 + 1) * 128)
                    STT(yb[:, gs], banks[bk][:, (g8 % 4) * 128:(g8 % 4 + 1) * 128], bss[:, g8:g8 + 1], gu[k][:, gs],
                        ALU.add, ALU.mult, [PB(bk), RC, R("gu", k, g8 // 4)], [R("yb", g8)])
            TT("pool", mxb[k], yb, sg[0], ALU.mult,
               [R("yb", g8) for g8 in range(8)] + [R("sg", 0, 0), R("sg", 0, 1)], [R("mxb", 0)])
            DMA("sp", mix_d[i * 128:(i + 1) * 128, :], mxb[k], [R("mxb", 0)], [RM(i)], mxb_sem[k])
        SCH.barrier()
        ptr[0] = base_persist
        if stop_after <= 2:
            return finish(nc, SCH, esem, None)

        G = alloc([128, 32, 16])
        SPf = alloc([128, 128])
        SPb = alloc([128, 128])
        tmpe = alloc([128, 128])
        gsc = {nm: alloc([128, 128]) for nm in ["wF", "uF", "sF", "wB", "uB", "sB", "sF16", "sB16"]}
        RG = R("gates")
        for i in range(NT):
            MMG([(banks[0][:, i * 16:(i + 1) * 16], hT[:, kc, i * 128:(i + 1) * 128], wg[:, kc, :], kc == 0, kc == 7)
                 for kc in range(8)], [RH(i), R("wg")], [PB(0)])
        TT("dve", G.rearrange("p a b -> p (a b)"), banks[0][:, :], bifs, ALU.add, [PB(0), RC], [RG])
        for (SPx, c0, key) in [(SPf, 4, "SPf"), (SPb, 12, "SPb")]:
            ACTF(tmpe.rearrange("p (a b) -> p a b", a=32, b=4), G[:, :, c0:c0 + 4], AF.Exp, [RG], [R("tmpe")], scale=-1.0)
            ACTF(SPx, tmpe, AF.Ln, [R("tmpe")], [R(key)], bias=1.0)
        MMG([(banks[1][:, 0:128], maskF, SPf, True, True)], [R("SPf"), RC], [PB(1)])
        MMG([(banks[1][:, 128:256], maskB, SPb, True, True)], [R("SPb"), RC], [PB(1)])
        MMG([(banks[1][:, 256:384], ones, SPf, True, True)], [R("SPf"), RC], [PB(1)])
        MMG([(banks[1][:, 384:512], ones, SPb, True, True)], [R("SPb"), RC], [PB(1)])
        RGS = R("gsc")
        ACTF(gsc["wF"], banks[1][:, 0:128], AF.Exp, [PB(1)], [RGS], scale=-1.0)
        ACTF(gsc["wB"], banks[1][:, 128:256], AF.Exp, [PB(1)], [RGS], scale=-1.0)
        ACTF(gsc["sF"], banks[1][:, 256:384], AF.Exp, [PB(1)], [RGS], scale=-1.0)
        ACTF(gsc["sB"], banks[1][:, 384:512], AF.Exp, [PB(1)], [RGS], scale=-1.0)
        TS("dve", gsc["sF16"], gsc["sF"], 1.0 / 16.0, None, ALU.mult, None, [RGS], [R("gs16", 0)])
        TS("dve", gsc["sB16"], gsc["sB"], 1.0 / 16.0, None, ALU.mult, None, [RGS], [R("gs16", 1)])
        TT("dve", tmpe.rearrange("p (a b) -> p a b", a=32, b=4), banks[1][:, 0:128].rearrange("p (a b) -> p a b", a=32, b=4),
           G[:, :, 0:4], ALU.add, [PB(1), RG, R("SPf"), R("SPb")], [R("tmpe")])
        ACTF(gsc["uF"], tmpe, AF.Exp, [R("tmpe")], [RGS])
        TT("dve", tmpe.rearrange("p (a b) -> p a b", a=32, b=4), banks[1][:, 128:256].rearrange("p (a b) -> p a b", a=32, b=4),
           G[:, :, 8:12], ALU.add, [PB(1), RG, RGS], [R("tmpe")])
        ACTF(gsc["uB"], tmpe, AF.Exp, [R("tmpe")], [RGS])

        SCH.barrier()
        if stop_after <= 2.2:
            return finish(nc, SCH, esem, None)
        qT = alloc([128, 2, S], BF16)
        kT = alloc([128, 2, S], BF16)
        vext = alloc([128, 32, 258], BF16)
        E("pool", lambda e: e.memset(vext[:, :, 256:258], 1.0), [], [R("vones")])
        wqk2 = [wqk_top, alloc([128, 8, 512], BF16)]
        wv2 = [wv_top, alloc([128, 8, 256], BF16)]
        wog = alloc([128, 8, 512], BF16)
        Chat = {d_: alloc([128, 2, 257]) for d_ in "FB"}
        Cbf = {d_: alloc([128, 2, 258], BF16) for d_ in "FB"}
        base_p3 = ptr[0]
        zst = alloc([128, 4098])
        acc = alloc([128, 4096])
        end_conv = ptr[0]
        ptr[0] = base_p3
        PT = [alloc([128, 128], BF16) for _ in range(2)]
        kw = [alloc([128, 256], BF16) for _ in range(2)]
        nds = [alloc([128, 257]) for _ in range(2)]
        sml = alloc([128, 64 * 4])
        hfo = [alloc([128, 256]) for _ in range(4)]
        hfo_sem = [new_dsem() for _ in range(4)]
        hfi = [alloc([128, 256]) for _ in range(4)]
        hfi_sem = [new_dsem() for _ in range(4)]
        mbi = [alloc([128, 256], BF16) for _ in range(4)]
        mbi_sem = [new_dsem() for _ in range(4)]
        mxo = [alloc([128, 256], BF16) for _ in range(4)]
        mxo_sem = [new_dsem() for _ in range(4)]
        sgo = [alloc([128, 512]) for _ in range(2)]
        hs = [alloc([128, 256]) for _ in range(2)]
        y1 = [alloc([128, 256]) for _ in range(2)]
        y2 = [alloc([128, 256]) for _ in range(2)]
        junk3 = alloc([128, 256], BF16)
        ptr[0] = max(ptr[0], end_conv)

        def RHF(i):
            return R("hf_d", i)

        for h in range(4):
            hc = slice(h * 256, (h + 1) * 256)
            wqk = wqk2[h % 2]
            wv = wv2[h % 2]

            def load_qkv(hh_):
                for wi, (dst, c0) in enumerate([(wqk2[hh_ % 2][:, :, 0:256], hh_ * 256),
                                                (wqk2[hh_ % 2][:, :, 256:512], 1024 + hh_ * 256),
                                                (wv2[hh_ % 2], 2048 + hh_ * 256)]):
                    DMA("pool", dst, w_in[:, c0:c0 + 256].rearrange("(kc p) n -> p kc n", p=128), [],
                        [R("wh", hh_ % 2, wi)], None)

            for wi, (dst, c0) in enumerate([(wog[:, :, 0:256], 3072 + h * 256), (wog[:, :, 256:512], 6144 + h * 256)]):
                DMA("pool", dst, w_in[:, c0:c0 + 256].rearrange("(kc p) n -> p kc n", p=128), [], [R("wog", wi)], None)
            RWQK = [R("wh", h % 2, 0), R("wh", h % 2, 1)]
            RWV = [R("wh", h % 2, 2)]
            RWOG = [R("wog", 0), R("wog", 1)]
            E("pool", lambda e: e.memset(zst[:, 0:1], 0.0), [], [R("zpad")])
            E("pool", lambda e: e.memset(zst[:, 4097:4098], 0.0), [], [R("zpad")])
            for ct in range(4):
                cti = (0 if ct < 2 else 8) + h * 2 + (ct % 2)
                for tb in range(8):
                    bk = tb % 2
                    MMG([(banks[bk][:, 0:512], wqk[:, kc, ct * 128:(ct + 1) * 128], hT[:, kc, tb * 512:(tb + 1) * 512],
                          kc == 0, kc == 7) for kc in range(8)], [RH(4 * tb + j) for j in range(4)] + RWQK, [PB(bk)])
                    if tb % 2 == 0:
                        ACTF(zst[:, 1 + tb * 512:1 + (tb + 1) * 512], banks[bk][:, :], AF.Copy, [PB(bk)], [R("z", tb)])
                    else:
                        E("dve", lambda e, o_=zst[:, 1 + tb * 512:1 + (tb + 1) * 512], i_=banks[bk][:, :]: e.tensor_copy(o_, i_),
                          [PB(bk)], [R("z", tb)], 0.65)
                for hf2 in range(2):
                    a0 = hf2 * 2048
                    zr = [R("z", tb) for tb in range(8)] + [R("zpad")]
                    ACTF(acc[:, a0:a0 + 2048], zst[:, 1 + a0:1 + a0 + 2048], AF.Identity, zr + [RC], [R("acc", hf2)],
                         scale=cws[:, cti * 3 + 1:cti * 3 + 2], bias=cbs[:, cti:cti + 1])
                    STT(acc[:, a0:a0 + 2048], zst[:, a0:a0 + 2048], cws[:, cti * 3:cti * 3 + 1], acc[:, a0:a0 + 2048],
                        ALU.mult, ALU.add, zr + [RC, R("acc", hf2)], [R("acc", hf2)])
                    STT(acc[:, a0:a0 + 2048], zst[:, 2 + a0:2 + a0 + 2048], cws[:, cti * 3 + 2:cti * 3 + 3], acc[:, a0:a0 + 2048],
                        ALU.mult, ALU.add, zr + [RC, R("acc", hf2)], [R("acc", hf2)])
                    dst = (qT if ct < 2 else kT)[:, ct % 2, a0:a0 + 2048]
                    rdst = [R("qk", ct, 16 * hf2 + j) for j in range(16)]
                    ACTF(dst, acc[:, a0:a0 + 2048], AF.Silu, [R("acc", hf2)], rdst)
            for i in range(NT):
                bk = 2 + i % 2
                MMG([(banks[bk][:, 0:256], hT[:, kc, i * 128:(i + 1) * 128], wv[:, kc, :], kc == 0, kc == 7)
                     for kc in range(8)], [RH(i)] + RWV, [PB(bk)])
                ACTF(vext[:, i, 0:256], banks[bk][:, 0:256], AF.Copy, [PB(bk)], [R("v", i)])
            SCH.barrier()
            if stop_after <= 2.4:
                return finish(nc, SCH, esem, None)

            def sweep_tile(i, dirn, first, prev_i, fin):
                par = 0 if dirn == "F" else 1
                P0, P1, P2, P3 = [4 * par + j for j in range(4)]
                col = i * 4 + h
                uX = gsc["u" + dirn][:, col:col + 1]
                wX = gsc["w" + dirn][:, col:col + 1]
                msk = maskFq if dirn == "F" else maskBq
                ts_ = slice(i * 128, (i + 1) * 128)
                rq = [R("qk", 0, i), R("qk", 1, i)]
                rk = [R("qk", 2, i), R("qk", 3, i)]
                RCh = R("Chat", dirn)
                RCb = R("Cbf", dirn)
                MMG([(banks[P0][:, 0:128], kT[:, j, ts_], qT[:, j, ts_], j == 0, j == 1) for j in range(2)],
                    rq + rk, [PB(P0)])
                TRG([(bank_bf(P0)[:, 256 + j * 128:256 + (j + 1) * 128], kT[:, j, ts_]) for j in range(2)],
                    rk + [RC], [PB(P0)])
                STT(PT[par], banks[P0][:, 0:128], uX, msk, ALU.mult, ALU.mult, [PB(P0), RGS, RC], [R("PT", par)])
                ACTF(kw[par], bank_bf(P0)[:, 256:512], AF.Copy, [PB(P0), RGS], [R("kw", par)], scale=uX)
                items = []
                if not first:
                    sp_ = gsc["s" + dirn][:, prev_i * 4 + h:prev_i * 4 + h + 1]
                    sp16 = gsc["s" + dirn + "16"][:, prev_i * 4 + h:prev_i * 4 + h + 1]
                    TS("pool", Cbf[dirn][:, 0, 0:257], Chat[dirn][:, 0, :], sp_, 1.0 / 16.0, ALU.mult, ALU.mult,
                       [RCh, R("Chn", dirn), RGS], [R("Cbf", dirn, 0)])
                    ACTF(Cbf[dirn][:, 1, 0:257], Chat[dirn][:, 1, :], AF.Copy, [RCh, R("Chn", dirn), RGS, R("gs16", 0), R("gs16", 1)], [R("Cbf", dirn, 1)], scale=sp16)
                    items += [(banks[P1][:, 0:257], qT[:, j, ts_], Cbf[dirn][:, j, 0:257], j == 0, False) for j in range(2)]
                items += [(banks[P1][:, 0:257], PT[par], vext[:, i, 0:257], first, True)]
                MMG(items, rq + [R("Cbf", dirn, 0), R("Cbf", dirn, 1), R("PT", par), R("v", i), R("vones")], [PB(P1)])
                MMG([(banks[P1][:, 260 + j:261 + j], kw[par][:, j * 128:(j + 1) * 128], vext[:, i, 256:257], True, True)
                     for j in range(2)], [R("kw", par), R("vones")], [PB(P1)])
                MMG([(banks[P2][:, j * 256:(j + 1) * 256], kw[par][:, j * 128:(j + 1) * 128], vext[:, i, 0:256], True, True)
                     for j in range(2)], [R("kw", par), R("v", i)], [PB(P2)])
                ACTF(nds[par], banks[P1][:, 0:257], AF.Copy, [PB(P1), RGS], [R("nds", par)], scale=wX)
                so = ((i % 16) * 2 + (0 if dirn == "F" else 1)) * 8
                dn = sml[:, so:so + 1]
                rd = sml[:, so + 1:so + 2]
                ACTF(dn, nds[par][:, 256:257], AF.Abs, [R("nds", par)], [R("dn", so)])
                TS("dve", rd, dn, 1.0, None, ALU.max, None, [R("dn", so)], [R("dn2", so)])
                E("dve", lambda e, o_=rd, i_=rd: e.reciprocal(o_, i_), [R("dn2", so)], [R("rd", so)], 0.2)
                dC3 = banks[P2][:, :].rearrange("p (a b) -> p a b", a=2, b=256)
                if first:
                    E("dve", lambda e, o_=Chat[dirn][:, :, 0:256], i_=dC3: e.tensor_copy(o_, i_), [PB(P2)], [RCh], 0.65)
                    E("dve", lambda e, o_=Chat[dirn][:, :, 256], i_=banks[P1][:, 260:262]: e.tensor_copy(o_, i_),
                      [PB(P1)], [R("Chn", dirn)], 0.15)
                else:
                    STT(Chat[dirn][:, :, 0:256], Chat[dirn][:, :, 0:256], sp_, dC3, ALU.mult, ALU.add,
                        [RCh, RGS, PB(P2)], [RCh])
                    STT(Chat[dirn][:, :, 256], Chat[dirn][:, :, 256], sp_, banks[P1][:, 260:262], ALU.mult, ALU.add,
                        [R("Chn", dirn), RGS, PB(P1)], [R("Chn", dirn)])
                if not fin:
                    k3 = 2 * par + i % 2
                    ACTF(hfo[k3], nds[par][:, 0:256], AF.Copy, [R("nds", par), R("rd", so)], [R("hfo", k3)], scale=rd)
                    DMA("sp", hf_d[ts_, :], hfo[k3], [R("hfo", k3)], [RHF(i)], hfo_sem[k3])
                    return
                k3 = 2 * par + i % 2
                MMG([(banks[P3][:, 0:512], hT[:, kc, ts_], wog[:, kc, :], kc == 0, kc == 7) for kc in range(8)],
                    [RH(i)] + RWOG, [PB(P3)])
                ACTF(sgo[par], banks[P3][:, :], AF.Sigmoid, [PB(P3)], [R("sgo", par)])
                DMA("sp", hfi[k3], hf_d[ts_, :], [RHF(i)], [R("hfi", k3)], hfi_sem[k3])
                STT(hs[par], nds[par][:, 0:256], rd, hfi[k3], ALU.mult, ALU.add,
                    [R("nds", par), R("rd", so), R("hfi", k3)], [R("hs", par)])
                sq = sml[:, so + 2:so + 3]
                ve = sml[:, so + 3:so + 4]
                rs = sml[:, so + 4:so + 5]
                ACTF(junk3, hs[par], AF.Square, [R("hs", par)], [R("junk3"), R("sq", so)], accum_out=sq)
                TS("dve", ve, sq, 1.0 / 256.0, EPS, ALU.mult, ALU.add, [R("sq", so)], [R("ve3", so)])
                RSTD(ve, rs, R("ve3", so), R("rs3", so))
                TT("pool", y2[par], sgo[par][:, 0:256], sgo[par][:, 256:512], ALU.mult, [R("sgo", par)], [R("y2", par)])
                TT("pool", y2[par], y2[par], mhs[:, hc], ALU.mult, [R("y2", par), RP3], [R("y2", par)])
                STT(mxo[k3], hs[par], rs, y2[par], ALU.mult, ALU.mult, [R("hs", par), R("rs3", so), R("y2", par)], [R("mxo", k3)])
                DMA("sp", mixa_d[ts_, hc], mxo[k3], [R("mxo", k3)], [R("mixa_d", i, h)], mxo_sem[k3])

            if h + 1 < 4:
                load_qkv(h + 1)
            for st_ in range(NT):
                sweep_tile(st_, "F", st_ == 0, st_ - 1, st_ >= NT // 2)
                sweep_tile(NT - 1 - st_, "B", st_ == 0, NT - st_, st_ >= NT // 2)
            SCH.barrier()
            if stop_after <= 2.8:
                return finish(nc, SCH, esem, None)
        ptr[0] = 0
        if stop_after <= 3:
            return finish(nc, SCH, esem, None)

        ptr[0] = base_persist - 8 * S * 2
        top[0] = ARENA_WORDS * 4
        w1s = alloc([128, 8, 4 * D], BF16)
        w2s = alloc([128, 32, D], BF16)
        wos = alloc([128, 8, D], BF16)
        wosem = new_dsem()
        wfsem = new_dsem()
        RWO = R("wo")
        RWF = R("wf")
        DMA("pool", wos, w_out.rearrange("(kc p) n -> p kc n", p=128), [], [R("wos")], None)
        RW1 = [R("w1s", 0), R("w1s", 1)]
        RW2 = [R("w2s", 0), R("w2s", 1)]
        for c in range(2):
            DMA("pool", w1s[:, :, c * 2048:(c + 1) * 2048],
                w1[:, c * 2048:(c + 1) * 2048].rearrange("(kc p) n -> p kc n", p=128), [], [RW1[c]], None)
        for c in range(2):
            DMA("pool", w2s[:, c * 16:(c + 1) * 16, :],
                w2[c * 2048:(c + 1) * 2048, :].rearrange("(kc p) n -> p kc n", p=128), [], [RW2[c]], None)
        g1s = alloc([128, D])
        g2s = alloc([128, D])
        nfs = alloc([128, D])
        A2p = alloc([128, 8])
        B2p = alloc([128, 8])
        DMA("sp", g1s, mod_d[2048:3072].partition_broadcast(128), [RMOD], [RWO], wosem)
        DMA("sp", g2s, mod_d[5120:6144].partition_broadcast(128), [RMOD], [RWF], wfsem)
        DMA("sp", nfs, nfgb, [], [RWF], wfsem)
        DMA("sp", A2p, mod_d[4096:5120].rearrange("(c p) -> p c", p=128), [RMOD], [RWF], wfsem,
            allow_slow_non_contiguous=True)
        DMA("sp", B2p, mod_d[3072:4096].rearrange("(c p) -> p c", p=128), [RMOD], [RWF], wfsem,
            allow_slow_non_contiguous=True)
        STT(A2p, A2p, 1.0, n2gs, ALU.add, ALU.mult, [RWF, RC], [R("A2p")])
        base_p4 = ptr[0]
        mxi = [alloc([128, D], BF16) for _ in range(3)]
        mxi_sem = [new_dsem() for _ in range(3)]
        xi = [alloc([128, D]) for _ in range(3)]
        xi_sem = [new_dsem() for _ in range(3)]
        mT = [alloc([128, 8, 128], BF16) for _ in range(2)]
        tpj = [alloc([128, D]) for _ in range(2)]
        x1o = [alloc([128, D]) for _ in range(2)]
        x1o_sem = [new_dsem() for _ in range(2)]
        mai = [alloc([128, D], BF16) for _ in range(3)]
        mai_sem = [new_dsem() for _ in range(3)]

        def RX1(i):
            return R("x1_d", i)

        for i in range(NT):
            k = i % 2
            k3 = i % 3
            ts_ = slice(i * 128, (i + 1) * 128)
            DMA("sp", mxi[k3], mix_d[ts_, :], [RM(i)], [R("mxi", k3)], mxi_sem[k3])
            DMA("sp", xi[k3], x[ts_, :], [], [R("xi", k3)], xi_sem[k3])
            DMA("sp", mai[k3], mixa_d[ts_, :], [R("mixa_d", i, hh_) for hh_ in range(4)], [R("mai", k3)], mai_sem[k3])
            TT("dve", mxi[k3], mxi[k3], mai[k3], ALU.add, [R("mxi", k3), R("mai", k3)], [R("mxi", k3)])
            TRG([(bank_bf(k)[:, kc * 128:(kc + 1) * 128], mxi[k3][:, kc * 128:(kc + 1) * 128]) for kc in range(8)],
                [R("mxi", k3), RC], [PB(k)])
            ACTF(mT[k], bank_bf(k).rearrange("p (a b) -> p a b", a=8, b=128), AF.Copy, [PB(k)], [R("mT", k)])
            for hh in range(2):
                bk = 2 + 2 * k + hh
                cs = slice(hh * 512, (hh + 1) * 512)
                MMG([(banks[bk][:, 0:512], mT[k][:, kc, :], wos[:, kc, cs], kc == 0, kc == 7) for kc in range(8)],
                    [R("mT", k), R("wos")], [PB(bk)])
                TT("dve", tpj[k][:, cs], banks[bk][:, :], g1s[:, cs], ALU.mult, [PB(bk), RWO], [R("tpj", k, hh)])
            TT("pool", x1o[k], tpj[k], xi[k3], ALU.add, [R("tpj", k, 0), R("tpj", k, 1), R("xi", k3)], [R("x1o", k)])
            DMA("sp", x1_d[ts_, :], x1o[k], [R("x1o", k)], [RX1(i)], x1o_sem[k])
        SCH.barrier()
        ptr[0] = base_p4
        if stop_after <= 4:
            return finish(nc, SCH, esem, None)

        xg = [alloc([128, 2, D]) for _ in range(2)]
        xg_sem = [[new_dsem() for _ in range(2)] for _ in range(2)]
        h2t = [alloc([128, D], BF16) for _ in range(2)]
        h2T = [alloc([128, 8, 256], BF16) for _ in range(2)]
        sq4 = [alloc([128, 256]) for _ in range(4)]
        fT = [alloc([128, 256], BF16) for _ in range(4)]
        tf = [alloc([128, D]) for _ in range(2)]
        ob_sem = [new_dsem() for _ in range(2)]
        junk4 = alloc([128, D], BF16)
        sm4 = alloc([128, 256])
        out_ops = []
        NG = S // 256
        for g in range(NG):
            kg = g % 2
            for s_ in range(2):
                i = 2 * g + s_
                ts_ = slice(i * 128, (i + 1) * 128)
                so = (i % 32) * 8
                DMA("sp", xg[kg][:, s_, :], x1_d[ts_, :], [RX1(i)], [R("xg", kg, s_)], xg_sem[kg][s_])
                ACTF(h2t[s_], xg[kg][:, s_, :], AF.Square, [R("xg", kg, s_)], [R("h2t", s_), R("s4", so)],
                     accum_out=sm4[:, so:so + 1])
                TS("dve", sm4[:, so + 1:so + 2], sm4[:, so:so + 1], 1.0 / D, EPS, ALU.mult, ALU.add, [R("s4", so)], [R("v4", so)])
                RSTD(sm4[:, so + 1:so + 2], sm4[:, so + 2:so + 3], R("v4", so), R("r4", so))
                TS("dve", h2t[s_], xg[kg][:, s_, :], sm4[:, so + 2:so + 3], None, ALU.mult, None,
                   [R("xg", kg, s_), R("r4", so)], [R("h2t", s_)])
                TRG([(bank_bf(3)[:, kc * 128:(kc + 1) * 128], h2t[s_][:, kc * 128:(kc + 1) * 128]) for kc in range(8)],
                    [R("h2t", s_), RC], [PB(3)])
                for kc in range(8):
                    ACTF(h2T[kg][:, kc, s_ * 128:(s_ + 1) * 128], bank_bf(3)[:, kc * 128:(kc + 1) * 128], AF.Identity,
                         [PB(3), R("A2p"), RWF], [R("h2T", kg, s_)], scale=A2p[:, kc:kc + 1], bias=B2p[:, kc:kc + 1])
            rh2 = [R("h2T", kg, 0), R("h2T", kg, 1)]

            def ffn1(ft):
                bk = ft % 3
                MMG([(banks[bk][:, 0:256], w1s[:, kc, ft * 128:(ft + 1) * 128], h2T[kg][:, kc, :], kc == 0, kc == 7)
                     for kc in range(8)], rh2 + RW1, [PB(bk)])

            ffn1(0)
            for ft in range(32):
                bk = ft % 4
                pb = ft % 3
                if ft + 1 < 32:
                    ffn1(ft + 1)
                ACTF(sq4[bk], banks[pb][:, 0:256], AF.Square, [PB(pb)], [R("sq4", bk)])
                STT(fT[bk], banks[pb][:, 0:256], 0.0, sq4[bk], ALU.is_gt, ALU.mult, [PB(pb), R("sq4", bk)], [R("fT", bk)])
                for s_ in range(2):
                    for hh in range(2):
                        ob_ = 4 + s_ * 2 + hh
                        MMG([(banks[ob_][:, 0:512], fT[bk][:, s_ * 128:(s_ + 1) * 128], w2s[:, ft, hh * 512:(hh + 1) * 512],
                              ft == 0, ft == 31)], [R("fT", bk)] + RW2, [PB(ob_)])
            for s_ in range(2):
                i = 2 * g + s_
                ts_ = slice(i * 128, (i + 1) * 128)
                so = (i % 32) * 8
                for hh in range(2):
                    ob_ = 4 + s_ * 2 + hh
                    cs = slice(hh * 512, (hh + 1) * 512)
                    TT("dve", tf[s_][:, cs], banks[ob_][:, :], g2s[:, cs], ALU.mult, [PB(ob_), RWF], [R("tf", s_, hh)])
                TT("pool", xg[kg][:, s_, :], tf[s_], xg[kg][:, s_, :], ALU.add,
                   [R("tf", s_, 0), R("tf", s_, 1), R("xg", kg, s_)], [R("xg", kg, s_)])
                ACTF(junk4, xg[kg][:, s_, :], AF.Square, [R("xg", kg, s_)], [R("junk4"), R("s5", so)],
                     accum_out=sm4[:, so + 3:so + 4])
                TS("dve", sm4[:, so + 4:so + 5], sm4[:, so + 3:so + 4], 1.0 / D, EPS, ALU.mult, ALU.add, [R("s5", so)], [R("v5", so)])
                RSTD(sm4[:, so + 4:so + 5], sm4[:, so + 5:so + 6], R("v5", so), R("r5", so))
                STT(tf[s_], xg[kg][:, s_, :], sm4[:, so + 5:so + 6], nfs, ALU.mult, ALU.mult,
                    [R("xg", kg, s_), R("r5", so), RWF], [R("tf", s_, 0), R("tf", s_, 1)])
                out_ops.append(DMA("sp", out[ts_, :], tf[s_], [R("tf", s_, 0), R("tf", s_, 1)], [R("out", i)], ob_sem[s_]))
        return finish(nc, SCH, esem, out_ops)


def finish(nc, SCH, esem, out_ops):
    order = SCH.schedule()

    def replay(e, h):
        for o in order[e]:
            for d in o.waits:
                if d.dsem is not None:
                    h.wait_ge(d.dsem.handle, d.dval)
                else:
                    h.wait_ge(esem[d.eng], d.count)
            if o.fn is None:
                if o.signal:
                    h.nop().then_inc(esem[e], 1)
                continue
            ins = o.fn(h)
            if o.dsem is not None:
                ins.then_inc(o.dsem.handle, 16)
            elif o.signal:
                ins.then_inc(esem[e], 1)

    with nc.Block() as block:
        @block.tensor
        def _(h):
            replay("pe", h)

        @block.scalar
        def _(h):
            replay("act", h)

        @block.vector
        def _(h):
            replay("dve", h)

        @block.gpsimd
        def _(h):
            replay("pool", h)

        @block.sync
        def _(h):
            replay("sp", h)
    return nc


def host_inputs(b, x, c, w_ada, b_ada, norm1_g, norm2_g, w_in, b_if, conv_w, conv_b, mh_g,
                ln_v_g, ln_v_b, w_s, b_s, w_out, w1, w2, normf_g):
    f = np.float32

    def bc(v):
        return np.ascontiguousarray(np.broadcast_to(np.asarray(v, f).reshape(1, -1), (128, v.size)))

    r = np.arange(128)
    mF = (r[:, None] <= r[None, :]).astype(f)
    mB = (r[:, None] >= r[None, :]).astype(f)
    cm = np.concatenate([mF, mB, np.ones((128, 128), f), mF / 16.0, mB / 16.0], axis=1)
    cw = np.asarray(conv_w[0], f)
    cwp = np.ascontiguousarray(cw.reshape(3, 16, 128).transpose(2, 1, 0).reshape(128, 48))
    cbp = np.ascontiguousarray(np.asarray(conv_b[0], f).reshape(16, 128).T)
    return {
        "x": np.ascontiguousarray(x[b], dtype=f),
        "c_b": np.ascontiguousarray(np.asarray(c[b], f).reshape(8, 128).T),
        "w_ada": np.ascontiguousarray(w_ada[0], dtype=f),
        "b_ada": np.ascontiguousarray(b_ada[0:1], dtype=f),
        "n1gb": bc(norm1_g[0]),
        "n2gp": np.ascontiguousarray(np.asarray(norm2_g[0], f).reshape(8, 128).T),
        "w_in": np.ascontiguousarray(w_in[0], dtype=f),
        "bifb": np.ascontiguousarray(np.broadcast_to(np.asarray(b_if[0], f).reshape(1, 1, 16), (128, 32, 16)).reshape(128, 512)),
        "cwp": cwp,
        "cbp": cbp,
        "mhgb": bc(mh_g[0]),
        "lngb": bc(ln_v_g[0]),
        "lnbb": bc(ln_v_b[0]),
        "wsT": np.ascontiguousarray(np.asarray(w_s[0], f).transpose(2, 0, 1).reshape(128, 1024)),
        "bsp": np.ascontiguousarray(np.asarray(b_s[0], f).T),
        "w_out": np.ascontiguousarray(w_out[0], dtype=f),
        "w1": np.ascontiguousarray(w1[0], dtype=f),
        "w2": np.ascontiguousarray(w2[0], dtype=f),
        "nfgb": bc(normf_g),
        "cmask": cm,
        "identb": np.eye(128, dtype=f).astype(ml_dtypes.bfloat16),
    }


def kernel(**inputs):
    inputs = {k: np.asarray(v) for k, v in inputs.items()}
    nc = build_program()
    in_maps = [host_inputs(b, **inputs) for b in range(8)]
    res = run_bass_kernel_spmd(nc, in_maps, core_ids=list(range(8)))
    return np.stack([np.asarray(r["out"], dtype=np.float32) for r in res.results], axis=0)
```
